# Optimizing a Trainium2 kernel written in Bass

```python
import math
import jax
import jax.numpy as jnp
from jax import lax
import numpy as np

D_MODEL = 4096
BATCH = 4
SEQ = 2048
DEPTH = 2
DEC_BATCH = 128
DEC_SEQ = 4
PAST_LEN = 16384
PAGE_SIZE = 128

N_EVEN = (DEPTH + 1) // 2
N_ODD = DEPTH // 2
GDN_HEADS = 16
GDN_DK = 128
GDN_DV = 128
GDN_CONV = 4
RET_HEADS = 8
RET_DK = 256
RET_DV = 256
ROPE_BASE = 10000.0
CHUNK = 64
POOL_WINDOWS = (2, 4, 8, 16)
POOL_GROUPS = 4
POOL_DG = D_MODEL // POOL_GROUPS
POOL_PAST = max(POOL_WINDOWS) - 1
D_FF = 11008
FFN_CONV = 3
PLE_DIM = 256
EPS = 1e-6

GDN_QK = GDN_HEADS * GDN_DK
GDN_V = GDN_HEADS * GDN_DV
GDN_CONV_CH = 2 * GDN_QK + GDN_V
RET_QK = RET_HEADS * RET_DK
RET_V = RET_HEADS * RET_DV
MIX_WIDTH = GDN_V + RET_V
OFF_A = GDN_CONV_CH
OFF_B = OFF_A + GDN_HEADS
OFF_Z = OFF_B + GDN_HEADS
OFF_RQ = OFF_Z + GDN_V
OFF_RK = OFF_RQ + RET_QK
OFF_RV = OFF_RK + RET_QK
OFF_RG = OFF_RV + RET_V
IN_COLS = OFF_RG + RET_V

kernel_name = 'hybrid_gdn_retention_pool_decoder_step'


def rmsnorm(x, gain):
    xf = x.astype(jnp.float32)
    y = xf * lax.rsqrt(jnp.mean(xf * xf, axis=-1, keepdims=True) + EPS)
    return (y * gain.astype(jnp.float32)).astype(x.dtype)


def l2norm(x):
    return x * lax.rsqrt(jnp.sum(x * x, axis=-1, keepdims=True) + 1e-6)


def causal_dwconv(x_ext, w, length):
    out = x_ext[:, 0:length] * w[0]
    for j in range(1, w.shape[0]):
        out = out + x_ext[:, j:j + length] * w[j]
    return out


def rotary(x, pos):
    half = x.shape[-1] // 2
    inv_freq = ROPE_BASE ** (-jnp.arange(half, dtype=jnp.float32) / half)
    ang = pos.astype(jnp.float32)[:, None] * inv_freq[None, :]
    cos = jnp.cos(ang)[None, :, None, :]
    sin = jnp.sin(ang)[None, :, None, :]
    x1, x2 = x[..., :half], x[..., half:]
    return jnp.concatenate([x1 * cos - x2 * sin, x1 * sin + x2 * cos], axis=-1)


def _to_chunks(t, c, n):
    pad = n * c - t.shape[2]
    t = jnp.pad(t, [(0, 0), (0, 0), (0, pad)] + [(0, 0)] * (t.ndim - 3))
    return t.reshape(t.shape[:2] + (n, c) + t.shape[3:])


def _from_chunks(o, length):
    n, b, h, c, d = o.shape
    return jnp.moveaxis(o, 0, 2).reshape(b, h, n * c, d)[:, :, :length]


def _decay_matrix(gc):
    c = gc.shape[-1]
    causal = jnp.tril(jnp.ones((c, c), dtype=bool))
    diff = gc[..., :, None] - gc[..., None, :]
    return jnp.exp(jnp.where(causal, diff, -jnp.inf))


def gated_delta_chunked(q, k, v, g, beta, s0):
    length = q.shape[2]
    c = min(CHUNK, length)
    n = -(-length // c)
    q, k, v = _to_chunks(q, c, n), _to_chunks(k, c, n), _to_chunks(v, c, n)
    g, beta = _to_chunks(g, c, n), _to_chunks(beta, c, n)
    gc = jnp.cumsum(g, axis=-1)
    decay = _decay_matrix(gc)
    strict = jnp.tril(jnp.ones((c, c), dtype=bool), -1)
    kk = jnp.einsum('bhnid,bhnjd->bhnij', k, k)
    lower = jnp.where(strict, beta[..., :, None] * decay * kk, 0.0) + jnp.eye(c, dtype=q.dtype)
    rhs = jnp.concatenate([beta[..., None] * v, (beta * jnp.exp(gc))[..., None] * k], axis=-1)
    sol = lax.linalg.triangular_solve(lower, rhs, left_side=True, lower=True)
    dv = v.shape[-1]
    w_v, w_k = sol[..., :dv], sol[..., dv:]
    qk = jnp.einsum('bhnid,bhnjd->bhnij', q, k) * decay
    q_dec = q * jnp.exp(gc)[..., None]
    k_end = k * jnp.exp(gc[..., -1:] - gc)[..., None]
    g_end = jnp.exp(gc[..., -1])

    def step(s, xs):
        w_v_n, w_k_n, qk_n, q_dec_n, k_end_n, g_end_n = xs
        u = w_v_n - jnp.einsum('bhck,bhkv->bhcv', w_k_n, s)
        o = jnp.einsum('bhck,bhkv->bhcv', q_dec_n, s) + jnp.einsum('bhij,bhjv->bhiv', qk_n, u)
        s = g_end_n[..., None, None] * s + jnp.einsum('bhck,bhcv->bhkv', k_end_n, u)
        return s, o

    xs = tuple(jnp.moveaxis(t, 2, 0) for t in (w_v, w_k, qk, q_dec, k_end, g_end))
    s, o = lax.scan(step, s0, xs)
    return _from_chunks(o, length), s


def retention_chunked(q, k, v, g, s0):
    length = q.shape[2]
    c = min(CHUNK, length)
    n = -(-length // c)
    q, k, v, g = (_to_chunks(t, c, n) for t in (q, k, v, g))
    gc = jnp.cumsum(g, axis=-1)
    qk = jnp.einsum('bhnid,bhnjd->bhnij', q, k) * _decay_matrix(gc)
    q_dec = q * jnp.exp(gc)[..., None]
    k_end = k * jnp.exp(gc[..., -1:] - gc)[..., None]
    g_end = jnp.exp(gc[..., -1])

    def step(s, xs):
        v_n, qk_n, q_dec_n, k_end_n, g_end_n = xs
        o = jnp.einsum('bhck,bhkv->bhcv', q_dec_n, s) + jnp.einsum('bhij,bhjv->bhiv', qk_n, v_n)
        s = g_end_n[..., None, None] * s + jnp.einsum('bhck,bhcv->bhkv', k_end_n, v_n)
        return s, o

    xs = tuple(jnp.moveaxis(t, 2, 0) for t in (v, qk, q_dec, k_end, g_end))
    s, o = lax.scan(step, s0, xs)
    return _from_chunks(o, length), s


def delta_retention_mixer(h, pos, conv_buf, s_gdn, s_ret, w_in, conv_w, a_log, dt_bias, out_norm, w_out):
    bsz, length, _ = h.shape
    f32 = jnp.float32
    proj = h @ w_in

    def heads(t, nh):
        return t.reshape(bsz, length, nh, -1)

    def bhl(t):
        return jnp.swapaxes(t, 1, 2)

    ext = jnp.concatenate([conv_buf.astype(proj.dtype), proj[..., :GDN_CONV_CH]], axis=1)
    qkv = jax.nn.silu(causal_dwconv(ext, conv_w, length)).astype(f32)
    q = l2norm(heads(qkv[..., :GDN_QK], GDN_HEADS)) * (GDN_DK ** -0.5)
    k = l2norm(heads(qkv[..., GDN_QK:2 * GDN_QK], GDN_HEADS))
    v = heads(qkv[..., 2 * GDN_QK:], GDN_HEADS)
    a = proj[..., OFF_A:OFF_B].astype(f32)
    g = -jnp.exp(a_log.astype(f32)) * jax.nn.softplus(a + dt_bias.astype(f32))
    beta = jax.nn.sigmoid(proj[..., OFF_B:OFF_Z].astype(f32))
    o_a, s_gdn_new = gated_delta_chunked(bhl(q), bhl(k), bhl(v), bhl(g), bhl(beta), s_gdn.astype(f32))
    o_a = bhl(o_a)
    z = heads(proj[..., OFF_Z:OFF_RQ], GDN_HEADS).astype(f32)
    o_a = o_a * lax.rsqrt(jnp.mean(o_a * o_a, axis=-1, keepdims=True) + EPS) * out_norm.astype(f32) * jax.nn.silu(z)

    qr = rotary(heads(proj[..., OFF_RQ:OFF_RK].astype(f32), RET_HEADS), pos)
    kr = rotary(heads(proj[..., OFF_RK:OFF_RV].astype(f32), RET_HEADS), pos) * (RET_DK ** -0.5)
    vr = heads(proj[..., OFF_RV:OFF_RG].astype(f32), RET_HEADS)
    log_gamma = jnp.log1p(-jnp.exp2(-5.0 - jnp.arange(RET_HEADS, dtype=f32)))
    g_r = jnp.broadcast_to(log_gamma[None, :, None], (bsz, RET_HEADS, length))
    o_b, s_ret_new = retention_chunked(bhl(qr), bhl(kr), bhl(vr), g_r, s_ret.astype(f32))
    o_b = bhl(o_b)
    mu = jnp.mean(o_b, axis=-1, keepdims=True)
    var = jnp.mean(jnp.square(o_b - mu), axis=-1, keepdims=True)
    gate_r = heads(proj[..., OFF_RG:IN_COLS], RET_HEADS).astype(f32)
    o_b = (o_b - mu) * lax.rsqrt(var + EPS) * jax.nn.silu(gate_r)

    mixed = jnp.concatenate([o_a.reshape(bsz, length, GDN_V), o_b.reshape(bsz, length, RET_V)], axis=-1)
    y = mixed.astype(h.dtype) @ w_out
    return y, ext[:, -(GDN_CONV - 1):], s_gdn_new, s_ret_new


def pool_mixer(h, pos, buf, w_group, scale):
    bsz, length, _ = h.shape
    ext = jnp.concatenate([buf.astype(h.dtype), h], axis=1)
    csum = jnp.cumsum(ext.astype(jnp.float32), axis=1)
    csum = jnp.concatenate([jnp.zeros_like(csum[:, :1]), csum], axis=1)
    hf = h.astype(jnp.float32)
    groups = []
    for gi, w in enumerate(POOL_WINDOWS):
        sl = slice(gi * POOL_DG, (gi + 1) * POOL_DG)
        hi = csum[:, POOL_PAST + 1:POOL_PAST + 1 + length, sl]
        lo = csum[:, POOL_PAST + 1 - w:POOL_PAST + 1 - w + length, sl]
        cnt = jnp.minimum(w, pos + 1).astype(jnp.float32)[None, :, None]
        groups.append((hi - lo) / cnt - hf[..., sl])
    pooled = jnp.stack(groups, axis=2)
    y = jnp.einsum('blgc,gcd->blgd', pooled, w_group.astype(jnp.float32)).reshape(bsz, length, D_MODEL)
    y = (y * scale.astype(jnp.float32)).astype(h.dtype)
    return y, ext[:, -POOL_PAST:]


def conv_ffn(h, buf, w_up, conv_w, conv_b, w_down):
    length = h.shape[1]
    up = h @ w_up
    ext = jnp.concatenate([buf.astype(up.dtype), up], axis=1)
    c = causal_dwconv(ext, conv_w, length) + conv_b
    gate, val = c[..., :D_FF], c[..., D_FF:]
    return (jax.nn.silu(gate) * val) @ w_down, ext[:, -(FFN_CONV - 1):]


def trunk(x, p, pos, gdn_conv_st, gdn_st, ret_st, pool_st, ffn_conv_st,
          norm_mix, norm_ffn, norm_ple, norm_final, w_in, gdn_conv_w, gdn_a_log, gdn_dt_bias,
          gdn_out_norm, w_out, pool_w, pool_scale, ffn_w_up, ffn_conv_w, ffn_conv_b, ffn_w_down,
          ple_w_gate, ple_w_proj):
    new_gdn_conv, new_gdn, new_ret, new_pool, new_ffn = [], [], [], [], []
    for i in range(DEPTH):
        j = i // 2
        h = rmsnorm(x, norm_mix[i])
        if i % 2 == 0:
            y, c_buf, s_a, s_b = delta_retention_mixer(
                h, pos, gdn_conv_st[j], gdn_st[j], ret_st[j], w_in[j], gdn_conv_w[j],
                gdn_a_log[j], gdn_dt_bias[j], gdn_out_norm[j], w_out[j])
            new_gdn_conv.append(c_buf.astype(gdn_conv_st.dtype))
            new_gdn.append(s_a.astype(gdn_st.dtype))
            new_ret.append(s_b.astype(ret_st.dtype))
        else:
            y, p_buf = pool_mixer(h, pos, pool_st[j], pool_w[j], pool_scale[j])
            new_pool.append(p_buf.astype(pool_st.dtype))
        x = x + y
        h = rmsnorm(x, norm_ffn[i])
        y, f_buf = conv_ffn(h, ffn_conv_st[i], ffn_w_up[i], ffn_conv_w[i], ffn_conv_b[i], ffn_w_down[i])
        new_ffn.append(f_buf.astype(ffn_conv_st.dtype))
        x = x + y
        gate = jax.nn.sigmoid(rmsnorm(x, norm_ple[i]) @ ple_w_gate[i])
        x = x + gate * (p[i] @ ple_w_proj[i])
    out = rmsnorm(x, norm_final)
    return out, (jnp.stack(new_gdn_conv), jnp.stack(new_gdn), jnp.stack(new_ret),
                 jnp.stack(new_pool), jnp.stack(new_ffn))


def setup_inputs(seed: int = 0) -> dict:
    key = jax.random.key(seed)
    ks = jax.random.split(key, 32)
    f32 = jnp.float32

    def nrm(k, shape, scale):
        return scale * jax.random.normal(k, shape, f32)

    def gain(k, shape):
        return 1.0 + nrm(k, shape, 0.02)

    dt = jnp.exp(jax.random.uniform(ks[16], (N_EVEN, GDN_HEADS), f32, math.log(1e-3), math.log(1e-1)))
    return {
        'x_prompt': nrm(ks[0], (BATCH, SEQ, D_MODEL), 1.0),
        'x_sample': nrm(ks[1], (DEC_BATCH, DEC_SEQ, D_MODEL), 1.0),
        'state_gdn_conv': nrm(ks[2], (N_EVEN, DEC_BATCH, GDN_CONV - 1, GDN_CONV_CH), 1.0),
        'state_gdn': nrm(ks[3], (N_EVEN, DEC_BATCH, GDN_HEADS, GDN_DK, GDN_DV), 0.1),
        'state_ret': nrm(ks[4], (N_EVEN, DEC_BATCH, RET_HEADS, RET_DK, RET_DV), 0.1),
        'state_pool': nrm(ks[5], (N_ODD, DEC_BATCH, POOL_PAST, D_MODEL), 1.0),
        'state_ffn_conv': nrm(ks[6], (DEPTH, DEC_BATCH, FFN_CONV - 1, 2 * D_FF), 1.0),
        'p_prompt': nrm(ks[7], (DEPTH, BATCH, SEQ, PLE_DIM), 1.0),
        'p_sample': nrm(ks[8], (DEPTH, DEC_BATCH, DEC_SEQ, PLE_DIM), 1.0),
        'norm_mix': gain(ks[9], (DEPTH, D_MODEL)),
        'norm_ffn': gain(ks[10], (DEPTH, D_MODEL)),
        'norm_ple': gain(ks[11], (DEPTH, D_MODEL)),
        'norm_final': gain(ks[12], (D_MODEL,)),
        'w_in': nrm(ks[13], (N_EVEN, D_MODEL, IN_COLS), D_MODEL ** -0.5),
        'gdn_conv_w': nrm(ks[14], (N_EVEN, GDN_CONV, GDN_CONV_CH), GDN_CONV ** -0.5),
        'gdn_a_log': jnp.log(jax.random.uniform(ks[15], (N_EVEN, GDN_HEADS), f32, 1.0, 16.0)),
        'gdn_dt_bias': dt + jnp.log(-jnp.expm1(-dt)),
        'gdn_out_norm': gain(ks[17], (N_EVEN, GDN_DV)),
        'w_out': nrm(ks[18], (N_EVEN, MIX_WIDTH, D_MODEL), MIX_WIDTH ** -0.5),
        'pool_w': nrm(ks[19], (N_ODD, POOL_GROUPS, POOL_DG, POOL_DG), POOL_DG ** -0.5),
        'pool_scale': gain(ks[20], (N_ODD, D_MODEL)),
        'ffn_w_up': nrm(ks[21], (DEPTH, D_MODEL, 2 * D_FF), D_MODEL ** -0.5),
        'ffn_conv_w': nrm(ks[22], (DEPTH, FFN_CONV, 2 * D_FF), FFN_CONV ** -0.5),
        'ffn_conv_b': nrm(ks[23], (DEPTH, 2 * D_FF), 0.02),
        'ffn_w_down': nrm(ks[24], (DEPTH, D_FF, D_MODEL), D_FF ** -0.5),
        'ple_w_gate': nrm(ks[25], (DEPTH, D_MODEL, D_MODEL), D_MODEL ** -0.5),
        'ple_w_proj': nrm(ks[26], (DEPTH, PLE_DIM, D_MODEL), PLE_DIM ** -0.5),
    }


def reference(x_prompt, x_sample, state_gdn_conv, state_gdn, state_ret, state_pool, state_ffn_conv,
              p_prompt, p_sample, norm_mix, norm_ffn, norm_ple, norm_final, w_in, gdn_conv_w,
              gdn_a_log, gdn_dt_bias, gdn_out_norm, w_out, pool_w, pool_scale, ffn_w_up, ffn_conv_w,
              ffn_conv_b, ffn_w_down, ple_w_gate, ple_w_proj):
    bp, lp, _ = x_prompt.shape
    z_gdn_conv = jnp.zeros((N_EVEN, bp) + state_gdn_conv.shape[2:], state_gdn_conv.dtype)
    z_gdn = jnp.zeros((N_EVEN, bp) + state_gdn.shape[2:], state_gdn.dtype)
    z_ret = jnp.zeros((N_EVEN, bp) + state_ret.shape[2:], state_ret.dtype)
    z_pool = jnp.zeros((N_ODD, bp) + state_pool.shape[2:], state_pool.dtype)
    z_ffn = jnp.zeros((DEPTH, bp) + state_ffn_conv.shape[2:], state_ffn_conv.dtype)
    pos_p = jnp.arange(lp, dtype=jnp.int32)
    pos_s = PAST_LEN + jnp.arange(x_sample.shape[1], dtype=jnp.int32)

    y_prompt, (gc_p, gd_p, rt_p, pl_p, ff_p) = trunk(
        x_prompt, p_prompt, pos_p, z_gdn_conv, z_gdn, z_ret, z_pool, z_ffn,
        norm_mix, norm_ffn, norm_ple, norm_final, w_in, gdn_conv_w, gdn_a_log, gdn_dt_bias,
        gdn_out_norm, w_out, pool_w, pool_scale, ffn_w_up, ffn_conv_w, ffn_conv_b, ffn_w_down,
        ple_w_gate, ple_w_proj)
    y_sample, (gc_s, gd_s, rt_s, pl_s, ff_s) = trunk(
        x_sample, p_sample, pos_s, state_gdn_conv, state_gdn, state_ret, state_pool, state_ffn_conv,
        norm_mix, norm_ffn, norm_ple, norm_final, w_in, gdn_conv_w, gdn_a_log, gdn_dt_bias,
        gdn_out_norm, w_out, pool_w, pool_scale, ffn_w_up, ffn_conv_w, ffn_conv_b, ffn_w_down,
        ple_w_gate, ple_w_proj)
    return (y_prompt, y_sample, gc_p, gd_p, rt_p, pl_p, ff_p, gc_s, gd_s, rt_s, pl_s, ff_s)
```

```python
import math
import numpy as np
import concourse.bass as bass
import concourse.mybir as mybir
from concourse.bass_utils import run_bass_kernel_spmd
from contextlib import ExitStack

F32 = mybir.dt.float32
BF16 = mybir.dt.bfloat16
ALU = mybir.AluOpType
AF = mybir.ActivationFunctionType

SEM_CAP = 30000
D = 4096
KC = 32
T = 1152
TPR = 1088
TP = 1024
NS = 16
DFF = 11008
NFC = 86
EPS = 1e-6
IN_COLS = 16416
CQ, CK, CV, CAB, CZ, CRQ, CRK, CRV, CRG = 0, 16, 32, 48, 49, 65, 81, 97, 113
NPJ = 129
PJ_COLS = [(i * 128, 128) for i in range(48)] + [(6144, 32)] + [(6176 + i * 128, 128) for i in range(16)] + \
          [(8224 + i * 128, 128) for i in range(64)]


class Ev:
    __slots__ = ("ctr", "sem", "val")

    def __init__(self, ctr, sem, val):
        self.ctr, self.sem, self.val = ctr, sem, val


class Counter:
    def __init__(self, K, is_dma=False):
        self.K, self.is_dma = K, is_dma
        self.sem = K.new_sem()
        self.val = 0

    def bump(self, inc):
        if self.val + inc > SEM_CAP:
            self.sem = self.K.new_sem()
            self.val = 0
        self.val += inc
        return Ev(self, self.sem, self.val)


class Buf:
    __slots__ = ("name", "w", "rs", "dctr", "excl")

    def __init__(self, name="", excl=False):
        self.name, self.w, self.rs, self.dctr, self.excl = name, None, {}, None, excl


class Eng:
    def __init__(self, K, h, name):
        self.K, self.h, self.name = K, h, name
        self.ctr = Counter(K)
        self.known = {}
        self.self_sync = name != "pe"

    def wait(self, ev):
        if ev is None:
            return
        val = ev.val
        if ev.ctr.is_dma and ev.ctr.sem is ev.sem:
            val = ev.ctr.val
        if ev.ctr is self.ctr and not self.self_sync:
            return
        key = id(ev.sem)
        if self.known.get(key, 0) >= val:
            return
        self.h.wait_ge(ev.sem, val)
        self.known[key] = val


class KB:
    def __init__(self, nc):
        self.nc = nc
        self.es = ExitStack()
        self.nsem = 0
        self.pe = Eng(self, nc.tensor, "pe")
        self.dve = Eng(self, nc.vector, "dve")
        self.act = Eng(self, nc.scalar, "act")
        self.pool = Eng(self, nc.gpsimd, "pool")
        self.sp = Eng(self, nc.sync, "sp")
        self.engs = [self.pe, self.dve, self.act, self.pool, self.sp]
        self.dma_ctrs, self.free_ctrs = [], []
        self.nins = 0
        self.uid = 0

    def new_sem(self):
        self.nsem += 1
        return self.es.enter_context(self.nc.semaphore(f"s{self.nsem}"))

    def get_dctr(self):
        if self.free_ctrs:
            return self.free_ctrs.pop()
        c = Counter(self, is_dma=True)
        self.dma_ctrs.append(c)
        return c

    def _deps(self, eng, reads, writes):
        for b in reads:
            eng.wait(b.w)
        for b in writes:
            eng.wait(b.w)
            for r in b.rs.values():
                eng.wait(r)

    def _commit(self, ev, reads, writes):
        for b in reads:
            b.rs[id(ev.ctr)] = ev
        for b in writes:
            b.w = ev
            b.rs = {}

    def op(self, eng, fn, reads=(), writes=()):
        if any(b.excl for b in reads):
            writes = list(writes) + [b for b in reads if b.excl]
            reads = [b for b in reads if not b.excl]
        self._deps(eng, reads, writes)
        ins = fn(eng.h)
        ev = eng.ctr.bump(1)
        ins.then_inc(ev.sem, 1)
        self._commit(ev, reads, writes)
        self.nins += 1
        return ev

    def dma(self, q, out, in_, reads=(), writes=(), cb=None):
        self._deps(q, reads, writes)
        if cb.dctr is None:
            cb.dctr = self.get_dctr()
        ins = q.h.dma_start(out=out, in_=in_)
        ev = cb.dctr.bump(16)
        ins.then_inc(ev.sem, 16)
        self._commit(ev, reads, writes)
        self.nins += 1
        return ev

    def release(self, bufs):
        for b in bufs:
            if b.dctr is not None:
                self.free_ctrs.append(b.dctr)
                b.dctr = None

    def barrier(self):
        evs = [Ev(e.ctr, e.ctr.sem, e.ctr.val) for e in self.engs if e.ctr.val > 0]
        evs += [Ev(c, c.sem, c.val) for c in self.dma_ctrs if c.val > 0]
        for e in self.engs:
            for ev in evs:
                if ev.ctr is not e.ctr:
                    e.wait(ev)


class Phase:
    def __init__(self, k):
        self.k = k
        self.es = ExitStack()
        self.bufs = []

    def __enter__(self):
        self.es.__enter__()
        return self

    def __exit__(self, *a):
        self.k.barrier()
        self.k.release(self.bufs)
        return self.es.__exit__(*a)

    def sb(self, shape, dt=F32, name="t"):
        self.k.uid += 1
        t = self.es.enter_context(self.k.nc.sbuf_tensor(f"{name}{self.k.uid}", list(shape), dt))
        b = Buf(name)
        self.bufs.append(b)
        return t, b

    def ps(self, shape, dt=F32, name="p"):
        self.k.uid += 1
        t = self.es.enter_context(self.k.nc.psum_tensor(f"{name}{self.k.uid}", list(shape), dt))
        b = Buf(name)
        self.bufs.append(b)
        return t, b

    def buf(self, name=""):
        b = Buf(name)
        self.bufs.append(b)
        return b


class Ring:
    def __init__(self, items):
        self.items, self.i = items, 0

    def get(self):
        it = self.items[self.i % len(self.items)]
        self.i += 1
        return it


class Prog:
    def __init__(self, nc, debug=(), stop_after=None):
        self.nc = nc
        self.k = KB(nc)
        self.debug = debug
        self.stop_after = stop_after
        self.io = {}

    def din(self, name, shape, dt=F32):
        self.io[name] = self.nc.dram_tensor(name, list(shape), dt, kind="ExternalInput").ap()
        return self.io[name]

    def dout(self, name, shape, dt=F32):
        self.io[name] = self.nc.dram_tensor(name, list(shape), dt, kind="ExternalOutput").ap()
        return self.io[name]

    def dscr(self, name, shape, dt=F32):
        kind = "ExternalOutput" if name in self.debug else "Internal"
        self.io[name] = self.nc.dram_tensor(name, list(shape), dt, kind=kind).ap()
        return self.io[name]

    def rmsnorm(self, ph, Xd, BX, Tn, ttiles, gain_t, gain_idx, hT, BhT, cst, out_f32=None):
        k = self.k
        nt = len(ttiles)
        pss, Bpss = ph.ps([128, nt, 512], F32, "nps")
        xs = Ring([ph.sb([128, Tn], F32, "xs") for _ in range(3)])
        sq = Ring([ph.sb([128, Tn], BF16, "sq") for _ in range(2)])
        rstd, Brstd = ph.sb([128, Tn], F32, "rstd")
        for c in range(KC):
            x_t, Bx = xs.get()
            k.dma(k.sp, x_t[:], Xd[c], reads=[BX[c]], writes=[Bx], cb=Bx)
            s_t, Bs = sq.get()
            k.op(k.act, lambda h: h.activation(out=s_t[:], in_=x_t[:], func=AF.Square), reads=[Bx], writes=[Bs])
            for ti, (t0, tw) in enumerate(ttiles):
                k.op(k.pe, lambda h: h.matmul(pss[:, ti, 0:tw], lhsT=cst["ones_bf"][:], rhs=s_t[:, t0:t0 + tw],
                                              start=(c == 0), stop=(c == KC - 1)), reads=[Bs, cst["B"]], writes=[Bpss])
        for ti, (t0, tw) in enumerate(ttiles):
            k.op(k.act, lambda h: h.activation(out=rstd[:, t0:t0 + tw], in_=pss[:, ti, 0:tw], func=AF.Sqrt,
                                               bias=cst["eps"][:, 0:1], scale=1.0 / D), reads=[Bpss, cst["B"]], writes=[Brstd])
        k.op(k.dve, lambda h: h.reciprocal(out=rstd[:], in_=rstd[:]), reads=[Brstd], writes=[Brstd])
        for c in range(KC):
            x_t, Bx = xs.get()
            k.dma(k.sp, x_t[:], Xd[c], reads=[BX[c]], writes=[Bx], cb=Bx)
            if out_f32 is None:
                k.op(k.dve, lambda h: h.scalar_tensor_tensor(out=hT[:, c, :], in0=x_t[:], scalar=gain_t[:, gain_idx, c:c + 1],
                                                             in1=rstd[:], op0=ALU.mult, op1=ALU.mult),
                     reads=[Bx, Brstd, cst["B"]], writes=[BhT])
            else:
                out_f32(c, x_t, Bx, rstd, Brstd)

    def gemm(self, ph, Wd, nK, rhs_fn, Brhs, ttiles, blocks, chunks, epilogue, nslot=2, NB=512, extra=None):
        k = self.k
        nt = len(ttiles)
        slots = [ph.sb([128, nK, NB], BF16, "w") for _ in range(nslot)]
        pss = Ring([ph.ps([128, nt, 512], F32, "gps") for _ in range(2 if extra is None else 1)])
        Wv = Wd.rearrange("(kc p) n -> p kc n", p=128)
        if extra is not None:
            Wd2, nK2, rhs_fn2, Brhs2 = extra
            slots2 = [ph.sb([128, nK2, NB], BF16, "w2") for _ in range(nslot)]
            pss2 = Ring([ph.ps([128, nt, 512], F32, "gps2") for _ in range(1)])
            Wv2 = Wd2.rearrange("(kc p) n -> p kc n", p=128)
        loaded = {}

        def load(bi):
            if bi in loaded or bi >= len(blocks):
                return
            c0, ncol = blocks[bi]
            w_t, Bw = slots[bi % nslot]
            step = max(1, min(nK, 4096 // ncol))
            for k0 in range(0, nK, step):
                k1 = min(nK, k0 + step)
                k.dma(k.pool, w_t[:, k0:k1, 0:ncol], Wv[:, k0:k1, c0:c0 + ncol], writes=[Bw], cb=Bw)
            ent = [w_t, Bw, None, None]
            if extra is not None:
                w2, Bw2 = slots2[bi % nslot]
                k.dma(k.pool, w2[:, :, 0:ncol], Wv2[:, :, c0:c0 + ncol], writes=[Bw2], cb=Bw2)
                ent[2], ent[3] = w2, Bw2
            loaded[bi] = ent

        last_use = {}
        for pos_, ch_ in enumerate(chunks):
            last_use[ch_[0]] = pos_
        nxt = [0]

        def prefetch(pos_):
            while nxt[0] < len(blocks) and (nxt[0] < nslot or last_use[nxt[0] - nslot] < pos_):
                load(nxt[0])
                nxt[0] += 1

        for pos_, (bi, off, width, tag) in enumerate(chunks):
            prefetch(pos_)
            w_t, Bw, w2, Bw2 = loaded[bi]
            ps, Bps = pss.get()
            for kc in range(nK):
                for ti, (t0, tw) in enumerate(ttiles):
                    k.op(k.pe, lambda h: h.matmul(ps[0:width, ti, 0:tw], lhsT=w_t[:, kc, off:off + width], rhs=rhs_fn(kc, t0, tw),
                                                  start=(kc == 0), stop=(kc == nK - 1)), reads=[Bw, Brhs], writes=[Bps])
            if extra is None:
                epilogue(tag, ps, Bps, width)
            else:
                ps2, Bps2 = pss2.get()
                for kc in range(nK2):
                    for ti, (t0, tw) in enumerate(ttiles):
                        k.op(k.pe, lambda h: h.matmul(ps2[0:width, ti, 0:tw], lhsT=w2[:, kc, off:off + width], rhs=rhs_fn2(kc, t0, tw),
                                                      start=(kc == 0), stop=(kc == nK2 - 1)), reads=[Bw2, Brhs2], writes=[Bps2])
                epilogue(tag, ps, Bps, width, ps2, Bps2)

    def make_xupd(self, ph, Xs, BXs, Xd, BXd, Tn, ttiles, scale_t=None, scale_idx=0, cst=None, t_off=0, gate=False):
        k = self.k
        nt = len(ttiles)
        tw = ttiles[0][1]
        stg = Ring([ph.sb([128, Tn], F32, "xu") for _ in range(3)])
        gt = Ring([ph.sb([128, Tn], F32, "gt") for _ in range(2)]) if gate else None

        def epi(tag, ps, Bps, width, ps2=None, Bps2=None):
            x_t, Bx = stg.get()
            k.dma(k.sp, x_t[:], Xs[tag][:, t_off:t_off + Tn], reads=[BXs[tag]], writes=[Bx], cb=Bx)
            xv = x_t[:].rearrange("p (a b) -> p a b", a=nt)
            if gate:
                g_t, Bg = gt.get()
                gv = g_t[:].rearrange("p (a b) -> p a b", a=nt)
                k.op(k.act, lambda h: h.activation(out=gv, in_=ps[:, :, 0:tw], func=AF.Sigmoid), reads=[Bps], writes=[Bg])
                k.op(k.dve, lambda h: h.tensor_tensor(out=gv, in0=ps2[:, :, 0:tw], in1=gv, op=ALU.mult), reads=[Bps2, Bg], writes=[Bg])
                k.op(k.dve, lambda h: h.tensor_tensor(out=x_t[:], in0=x_t[:], in1=g_t[:], op=ALU.add), reads=[Bx, Bg], writes=[Bx])
            elif scale_t is None:
                k.op(k.dve, lambda h: h.tensor_tensor(out=xv, in0=ps[:, :, 0:tw], in1=xv, op=ALU.add), reads=[Bps, Bx], writes=[Bx])
            else:
                k.op(k.dve, lambda h: h.scalar_tensor_tensor(out=xv, in0=ps[:, :, 0:tw], scalar=scale_t[:, scale_idx, tag:tag + 1],
                                                             in1=xv, op0=ALU.mult, op1=ALU.add), reads=[Bps, Bx, cst["B"]], writes=[Bx])
            k.dma(k.sp, Xd[tag][:, t_off:t_off + Tn], x_t[:], reads=[Bx], writes=[BXd[tag]], cb=Bx)

        return epi


def merge_blocks(cols, maxw=512):
    blocks, chunks = [], []
    for idx, (c0, w) in enumerate(cols):
        if blocks and blocks[-1][0] + blocks[-1][1] == c0 and blocks[-1][1] + w <= maxw:
            b0, bn = blocks[-1]
            chunks.append((len(blocks) - 1, bn, w, idx))
            blocks[-1] = (b0, bn + w)
        else:
            blocks.append((c0, w))
            chunks.append((len(blocks) - 1, 0, w, idx))
    return blocks, chunks


TT_MAIN = [(0, 384), (384, 384), (768, 384)]
TT_PRE = [(0, 512), (512, 512)]
PRE_CH = list(range(16, 48)) + [48] + list(range(81, 113))
PRE_IDX = {c: i for i, c in enumerate(PRE_CH)}
LOGG = [math.log1p(-2.0 ** (-5 - h)) for h in range(8)]


def build(debug=(), stop_after=None):
    nc = bass.Bass("TRN2", target_bir_lowering=False)
    P = Prog(nc, debug, stop_after)
    k = P.k
    op, dma = k.op, k.dma
    xT = P.din("xT", [KC, 128, T]); xpT = P.din("xpT", [KC, 128, TP])
    pT = P.din("pT", [2, 2, 128, T])
    sgc = P.din("sgc", [48, 128, NS, 3]); sgd = P.din("sgd", [NS, 16, 128, 128]); srt = P.din("srt", [NS, 8, 256, 256])
    spl = P.din("spl", [KC, 128, NS, 15]); sff = P.din("sff", [2, 172, 128, NS, 2])
    gains = P.din("gains", [128, 8, KC])
    w_in = P.din("w_in", [D, IN_COLS]); w_out = P.din("w_out", [D, D]); pool_w = P.din("pool_w", [4, 1024, 1024])
    w_up = P.din("w_up", [2, D, 2 * DFF]); w_down = P.din("w_down", [2, DFF, D])
    w_pg = P.din("w_pg", [2, D, D]); w_pp = P.din("w_pp", [2, 256, D])
    gcw = P.din("gcw", [128, 48, 4]); fcw = P.din("fcw", [128, 2, 172, 4])
    ghv = P.din("ghv", [1, 48])
    gon = P.din("gon", [128, 1])
    masks = P.din("masks", [6, 128, 128])
    rowmask = P.din("rowmask", [128, NS])
    rdt = P.din("rdt", [8, 2, 128, 128])
    rtab = P.din("rtab", [8, 2, T])
    rkend = P.din("rkend", [128, 8, 9])
    rkendp = P.din("rkendp", [128, 8, 8])
    cs_m = P.din("cs_m", [2, 128, T]); cs_p = P.din("cs_p", [2, 128, TP])
    pinv = P.din("pinv", [1, 4 * T])
    hmask = P.din("hmask", [128, 1])
    YT = P.dout("YT", [KC, 128, T])
    GCP = P.dout("GCP", [48, 128, 3]); GCS = P.dout("GCS", [48, 128, NS, 3])
    GDP = P.dout("GDP", [16, 128, 128]); GDS = P.dout("GDS", [NS, 16, 128, 128])
    RTP = P.dout("RTP", [8, 2, 128, 256]); RTS = P.dout("RTS", [NS, 8, 2, 128, 256])
    PLP = P.dout("PLP", [KC, 128, 15]); PLS = P.dout("PLS", [KC, 128, NS, 15])
    FFP = P.dout("FFP", [2, 172, 128, 2]); FFS = P.dout("FFS", [2, 172, 128, NS, 2])
    X = P.dscr("X", [KC, 128, T])
    PJ = P.dscr("PJ", [NPJ, 128, T]); PJP = P.dscr("PJP", [len(PRE_CH), 128, TP])
    MIXT = P.dscr("MIXT", [KC, 128, T], BF16); AT = P.dscr("AT", [NFC, 128, T], BF16)
    SPRE = P.dscr("SPRE", [16, 128, 128]); RPRE = P.dscr("RPRE", [8, 2, 128, 256])
    BXin = [Buf() for _ in range(KC)]; BX = [Buf() for _ in range(KC)]
    BPJ = [Buf() for _ in range(NPJ)]; BPJP = [Buf() for _ in range(len(PRE_CH))]
    BMIX = [Buf() for _ in range(KC)]; BAT = [Buf() for _ in range(NFC)]
    BSPRE = [Buf() for _ in range(16)]; BRPRE = [Buf() for _ in range(8)]
    Bnull = Buf()

    with k.es:
        nces = k.es
        def csb(shape, dt=F32, name="c"):
            k.uid += 1
            return nces.enter_context(nc.sbuf_tensor(f"{name}{k.uid}", list(shape), dt))
        Bc = Buf("const")
        cst = {"B": Bc}
        cst["ones_bf"] = csb([128, 128], BF16); cst["ones_f"] = csb([128, 128]); cst["ident"] = csb([128, 128])
        cst["eps"] = csb([128, 1]); gains_t = csb([128, 8, KC]); hm_t = csb([128, 1])
        op(k.dve, lambda h: h.memset(cst["ones_bf"][:], 1.0), writes=[Bc])
        op(k.dve, lambda h: h.memset(cst["ones_f"][:], 1.0), writes=[Bc])
        op(k.dve, lambda h: h.memset(cst["eps"][:], EPS), writes=[Bc])
        op(k.pool, lambda h: h.memset(cst["ident"][:], 1.0), writes=[Bc])
        op(k.pool, lambda h: h.affine_select(out=cst["ident"][:], in_=cst["ident"][:], pattern=[[-1, 128]], compare_op=ALU.is_equal,
                                             fill=0.0, base=0, channel_multiplier=1), reads=[Bc], writes=[Bc])
        dma(k.sp, gains_t[:], gains, writes=[Bc], cb=Bc)
        dma(k.sp, hm_t[:], hmask, writes=[Bc], cb=Bc)
        ident, ones_f = cst["ident"], cst["ones_f"]

        def evac_alt(i):
            return k.act if i % 2 == 0 else k.dve

        def copy_ps(eng, out, in_, reads, writes):
            if eng is k.act:
                op(k.act, lambda h: h.activation(out=out, in_=in_, func=AF.Copy), reads=reads, writes=writes)
            else:
                op(eng, lambda h: h.tensor_copy(out, in_), reads=reads, writes=writes)

        def proj_phase(Xd, BXd, Tn, tts, chunk_ids, OUT, BOUT, idxmap):
            with Phase(k) as ph:
                hT, BhT = ph.sb([128, KC, Tn], BF16, "hT")
                with Phase(k) as ph2:
                    P.rmsnorm(ph2, Xd, BXd, Tn, tts, gains_t, 0, hT, BhT, cst)
                blocks, chunks = merge_blocks([PJ_COLS[c] for c in chunk_ids])
                stg = Ring([ph.sb([128, Tn], F32, "pst") for _ in range(3)])
                nt, tw = len(tts), tts[0][1]
                cnt = [0]

                def epi(tag, ps, Bps, width):
                    s_t, Bs = stg.get()
                    cnt[0] += 1
                    sv = s_t[0:width, :].rearrange("p (a b) -> p a b", a=nt)
                    copy_ps(evac_alt(cnt[0]), sv, ps[0:width, :, 0:tw], [Bps], [Bs])
                    oi = idxmap(chunk_ids[tag])
                    dma(k.sp, OUT[oi][0:width, :], s_t[0:width, :], reads=[Bs], writes=[BOUT[oi]], cb=Bs)

                P.gemm(ph, w_in, KC, lambda kc, t0, tw_: hT[:, kc, t0:t0 + tw_], BhT, tts, blocks, chunks, epi, nslot=2, NB=512)

        proj_phase(xT, BXin, T, TT_MAIN, list(range(NPJ)), PJ, BPJ, lambda c: c)
        proj_phase(xpT, [Bnull] * KC, TP, TT_PRE, PRE_CH, PJP, BPJP, lambda c: PRE_IDX[c])

        def mixer_phase(main):
            Tn = T if main else TP
            ntile = 9 if main else 8
            tts = TT_MAIN if main else TT_PRE
            nt, tw = len(tts), tts[0][1]
            SRC, BSRC = (PJ, BPJ) if main else (PJP, BPJP)
            sidx = (lambda c: c) if main else (lambda c: PRE_IDX[c])
            with Phase(k) as ph:
                mk, Bmk = ph.sb([128, 6, 128], F32, "mk")
                dma(k.sp, mk[:], masks.rearrange("m p i -> p m i"), writes=[Bmk], cb=Bmk)
                rm, Brm = ph.sb([128, NS], F32, "rm")
                dma(k.sp, rm[:], rowmask, writes=[Brm], cb=Brm)
                cw, Bcw = ph.sb([128, 48, 4], F32, "cw")
                dma(k.sp, cw[:], gcw, writes=[Bcw], cb=Bcw)
                gh, Bgh = ph.sb([128, 48], F32, "gh")
                dma(k.sp, gh[:], ghv[0:1, :].partition_broadcast(128), writes=[Bgh], cb=Bgh)
                gon_t, Bgon = ph.sb([128, 1], F32, "gon")
                dma(k.sp, gon_t[:], gon, writes=[Bgon], cb=Bgon)
                wide, Bwide = ph.ps([128, 3, 512], F32, "wide")
                smalls = []
                for _ in range(5):
                    bank, _b = ph.ps([128, 512], F32, "bank")
                    _b.excl = True
                    smalls.append((bank[:, 0:128], _b))
                pring = Ring(smalls)
                tring = Ring([ph.sb([128, 128], F32, "tmp") for _ in range(20)])

                def pst():
                    return pring.get()

                def tmp():
                    t_, b_ = tring.get()
                    return t_[:], b_

                def mtype(t):
                    return 1 if (main and t == 8) else 0

                def MI(t): return mk[:, 3 * mtype(t) + 0, :]
                def MS(t): return mk[:, 3 * mtype(t) + 1, :]
                def BL(t): return mk[:, 3 * mtype(t) + 2, :]

                ab, Bab = ph.sb([32, Tn], F32, "ab")
                dma(k.sp, ab[:], SRC[sidx(CAB)][0:32, :], reads=[BSRC[sidx(CAB)]], writes=[Bab], cb=Bab)
                abt, Babt = ph.sb([128, ntile, 32], F32, "abt")
                for t in range(ntile):
                    p_, Bp_ = pst()
                    op(k.pe, lambda h: h.transpose(p_[:, 0:32], ab[0:32, t * 128:(t + 1) * 128], ident[0:32, 0:32]), reads=[Bab, Bc], writes=[Bp_])
                    copy_ps(k.act, abt[:, t, :], p_[:, 0:32], [Bp_], [Babt])
                tabs = {}
                for nm in ("g", "beta", "gc", "gl", "egc", "kend", "negegc"):
                    tabs[nm] = ph.sb([128, ntile, 16], F32, "tb" + nm)
                nea, Bnea = ph.sb([128, 16], F32, "nea")
                op(k.act, lambda h: h.activation(out=nea[:], in_=gh[:, 0:16], func=AF.Exp), reads=[Bgh], writes=[Bnea])
                op(k.dve, lambda h: h.tensor_scalar(out=nea[:], in0=nea[:], scalar1=-1.0, scalar2=None, op0=ALU.mult), reads=[Bnea], writes=[Bnea])
                g_t, Bg = tabs["g"]; be_t, Bbe = tabs["beta"]; gc_t, Bgc = tabs["gc"]; gl_t, Bgl = tabs["gl"]
                for t in range(ntile):
                    op(k.dve, lambda h: h.tensor_tensor(out=g_t[:, t, :], in0=abt[:, t, 0:16], in1=gh[:, 16:32], op=ALU.add), reads=[Babt, Bgh], writes=[Bg])
                op(k.act, lambda h: h.activation(out=g_t[:], in_=g_t[:], func=AF.Exp), reads=[Bg], writes=[Bg])
                op(k.act, lambda h: h.activation(out=g_t[:], in_=g_t[:], func=AF.Ln, bias=1.0), reads=[Bg], writes=[Bg])
                for t in range(ntile):
                    op(k.dve, lambda h: h.tensor_tensor(out=g_t[:, t, :], in0=g_t[:, t, :], in1=nea[:], op=ALU.mult), reads=[Bg, Bnea], writes=[Bg])
                op(k.act, lambda h: h.activation(out=be_t[:], in_=abt[:, :, 16:32], func=AF.Sigmoid), reads=[Babt], writes=[Bbe])
                for t in range(ntile):
                    p_, Bp_ = pst()
                    op(k.pe, lambda h: h.matmul(p_[:, 0:16], lhsT=MI(t), rhs=g_t[:, t, :], start=True, stop=True), reads=[Bmk, Bg], writes=[Bp_])
                    copy_ps(k.dve, gc_t[:, t, :], p_[:, 0:16], [Bp_], [Bgc])
                    p2, Bp2 = pst()
                    op(k.pe, lambda h: h.matmul(p2[:, 0:16], lhsT=BL(t), rhs=g_t[:, t, :], start=True, stop=True), reads=[Bmk, Bg], writes=[Bp2])
                    copy_ps(k.dve, gl_t[:, t, :], p2[:, 0:16], [Bp2], [Bgl])
                egc_t, Begc = tabs["egc"]; ke_t, Bke = tabs["kend"]; ne_t, Bne = tabs["negegc"]
                op(k.act, lambda h: h.activation(out=egc_t[:], in_=gc_t[:], func=AF.Exp), reads=[Bgc], writes=[Begc])
                op(k.dve, lambda h: h.tensor_scalar(out=ne_t[:], in0=egc_t[:], scalar1=-1.0, scalar2=None, op0=ALU.mult), reads=[Begc], writes=[Bne])
                op(k.dve, lambda h: h.tensor_tensor(out=ke_t[:], in0=gl_t[:], in1=gc_t[:], op=ALU.subtract), reads=[Bgl, Bgc], writes=[Bke])
                op(k.act, lambda h: h.activation(out=ke_t[:], in_=ke_t[:], func=AF.Exp), reads=[Bke], writes=[Bke])
                Btab = [Bg, Bbe, Bgc, Begc, Bke, Bne]

                EXW = 3 + TPR + NS * 7 if main else 3 + TP

                def conv_silu(ext, Bext, wi, out, Bout):
                    Lp = TPR if main else TP
                    op(k.dve, lambda h: h.tensor_scalar(out=out[:, 0:Lp], in0=ext[:, 0:Lp], scalar1=cw[:, wi, 0:1], scalar2=None, op0=ALU.mult),
                       reads=[Bext, Bcw], writes=[Bout])
                    for j in range(1, 4):
                        op(k.dve, lambda h: h.scalar_tensor_tensor(out=out[:, 0:Lp], in0=ext[:, j:j + Lp], scalar=cw[:, wi, j:j + 1], in1=out[:, 0:Lp],
                                                                   op0=ALU.mult, op1=ALU.add), reads=[Bext, Bcw, Bout], writes=[Bout])
                    if main:
                        es_ = ext[:, 3 + TPR:].rearrange("p (b j) -> p b j", j=7)
                        os_ = out[:, TPR:].rearrange("p (b j) -> p b j", j=4)
                        op(k.pool, lambda h: h.tensor_scalar(out=os_, in0=es_[:, :, 0:4], scalar1=cw[:, wi, 0:1], scalar2=None, op0=ALU.mult),
                           reads=[Bext, Bcw], writes=[Bout])
                        for j in range(1, 4):
                            op(k.dve, lambda h: h.scalar_tensor_tensor(out=os_, in0=es_[:, :, j:j + 4], scalar=cw[:, wi, j:j + 1], in1=os_,
                                                                        op0=ALU.mult, op1=ALU.add), reads=[Bext, Bcw, Bout], writes=[Bout])
                    op(k.act, lambda h: h.activation(out=out[:, :], in_=out[:, :], func=AF.Silu), reads=[Bout], writes=[Bout])

                def load_ext(ext, Bext, chunk, state_zero):
                    Lp = TPR if main else TP
                    ci = sidx(chunk)
                    if state_zero:
                        op(k.pool, lambda h: h.memset(ext[:, 0:3], 0.0), writes=[Bext])
                    else:
                        pi = PRE_IDX[chunk]
                        dma(k.sp, ext[:, 0:3], PJP[pi][:, TP - 3:TP], reads=[BPJP[pi]], writes=[Bext], cb=Bext)
                    dma(k.sp, ext[:, 3:3 + Lp], SRC[ci][:, 0:Lp], reads=[BSRC[ci]], writes=[Bext], cb=Bext)
                    if main:
                        es_ = ext[:, 3 + TPR:].rearrange("p (b j) -> p b j", j=7)
                        dma(k.sp, es_[:, :, 0:3], sgc[chunk], writes=[Bext], cb=Bext)
                        dma(k.sp, es_[:, :, 3:7], SRC[ci][:, TPR:].rearrange("p (b j) -> p b j", j=4), reads=[BSRC[ci]], writes=[Bext], cb=Bext)
                        dma(k.sp, GCP[chunk], ext[:, TPR:TPR + 3], reads=[Bext], cb=Bext)
                        dma(k.sp, GCS[chunk], es_[:, :, 4:7], reads=[Bext], cb=Bext)

                sqt, Bsqt = ph.sb([128, Tn], F32, "sqt")
                rnt, Brnt = ph.sb([128, Tn], F32, "rnt")

                def colsum_rs(x, Bx, scale_in, eps_ap, nparts_chunks=1):
                    op(k.act, lambda h: h.activation(out=sqt[:], in_=x, func=AF.Square), reads=[Bx], writes=[Bsqt])
                    for ti, (t0, tw_) in enumerate(tts):
                        op(k.pe, lambda h: h.matmul(wide[:, ti, 0:tw_], lhsT=ones_f[:], rhs=sqt[:, t0:t0 + tw_], start=True, stop=True), reads=[Bsqt, Bc], writes=[Bwide])
                    for ti, (t0, tw_) in enumerate(tts):
                        op(k.act, lambda h: h.activation(out=rnt[:, t0:t0 + tw_], in_=wide[:, ti, 0:tw_], func=AF.Sqrt, bias=eps_ap, scale=scale_in),
                           reads=[Bwide, Bc], writes=[Brnt])
                    op(k.dve, lambda h: h.reciprocal(out=rnt[:], in_=rnt[:]), reads=[Brnt], writes=[Brnt])

                exts = Ring([ph.sb([128, EXW], F32, "ext") for _ in range(3)])
                qT, BqT = ph.sb([128, Tn], F32, "qT"); kT, BkT = ph.sb([128, Tn], F32, "kT"); vT, BvT = ph.sb([128, Tn], F32, "vT")
                oT, BoT = ph.sb([128, Tn], F32, "oT")
                S, BS = ph.sb([128, 128], F32, "S")
                if main:
                    Sb, BSb = ph.sb([128, NS, 128], F32, "Sb"); Sbn, BSbn = ph.sb([128, NS, 128], F32, "Sbn")
                    zT, BzT = ph.sb([128, Tn], F32, "zT"); mixo, Bmixo = ph.sb([128, Tn], BF16, "mixo")
                LL = {nm: [ph.sb([128, 128], F32, "ll" + nm) for _ in range(2)] for nm in ("Q", "qkm", "kend", "vtok", "qdec", "egrow", "u")}

                for hd in range(16):
                    for (chunk, dst, Bdst, scale) in ((CQ + hd, qT, BqT, 128.0 ** -0.5), (CK + hd, kT, BkT, 1.0), (CV + hd, vT, BvT, None)):
                        if chunk < CK and not main:
                            continue
                        ext, Bext = exts.get()
                        load_ext(ext, Bext, chunk, state_zero=(not main) or chunk < CK)
                        conv_silu(ext, Bext, chunk, dst, Bdst)
                        if scale is not None:
                            colsum_rs(dst[:], Bdst, 1.0, cst["eps"][:, 0:1])
                            op(k.dve, lambda h: h.scalar_tensor_tensor(out=dst[:], in0=dst[:], scalar=float(scale), in1=rnt[:], op0=ALU.mult, op1=ALU.mult),
                               reads=[Bdst, Brnt], writes=[Bdst])
                    if main:
                        dma(k.sp, S[:], SPRE[hd], reads=[BSPRE[hd]], writes=[BS], cb=BS)
                        dma(k.sp, Sb[:], sgd[:, hd].rearrange("b p d -> p b d"), writes=[BSb], cb=BSb)
                        dma(k.sp, zT[:], PJ[CZ + hd], reads=[BPJ[CZ + hd]], writes=[BzT], cb=BzT)
                    else:
                        op(k.pool, lambda h: h.memset(S[:], 0.0), writes=[BS])

                    def pre(t):
                        par = t % 2
                        cs0 = t * 128
                        gcol = g_t[:, t, hd:hd + 1]; gccol = gc_t[:, t, hd:hd + 1]; bcol = be_t[:, t, hd:hd + 1]; kecol = ke_t[:, t, hd:hd + 1]
                        G, BG = tmp()
                        op(k.dve, lambda h: h.tensor_scalar(out=G, in0=ones_f[:], scalar1=gcol, scalar2=None, op0=ALU.mult), reads=[Bc, Bg], writes=[BG])
                        p_gc, Bpgc = pst()
                        op(k.pe, lambda h: h.matmul(p_gc, lhsT=G, rhs=MI(t), start=True, stop=True), reads=[BG, Bmk], writes=[Bpgc])
                        dm, Bdm = tmp()
                        op(k.dve, lambda h: h.tensor_scalar(out=dm, in0=p_gc, scalar1=gccol, scalar2=0.0, op0=ALU.subtract, op1=ALU.min), reads=[Bpgc, Bgc], writes=[Bdm])
                        op(k.act, lambda h: h.activation(out=dm, in_=dm, func=AF.Exp), reads=[Bdm], writes=[Bdm])
                        eg, Beg = LL["egrow"][par]
                        op(k.act, lambda h: h.activation(out=eg[:], in_=p_gc, func=AF.Exp), reads=[Bpgc], writes=[Beg])
                        DTs, BDTs = tmp()
                        op(k.pool, lambda h: h.tensor_tensor(out=DTs, in0=dm, in1=MS(t), op=ALU.mult), reads=[Bdm, Bmk], writes=[BDTs])
                        p_kk, Bpkk = pst()
                        op(k.pe, lambda h: h.matmul(p_kk, lhsT=kT[:, cs0:cs0 + 128], rhs=kT[:, cs0:cs0 + 128], start=True, stop=True), reads=[BkT], writes=[Bpkk])
                        M, BM = tmp()
                        op(k.dve, lambda h: h.scalar_tensor_tensor(out=M, in0=p_kk, scalar=bcol, in1=DTs, op0=ALU.mult, op1=ALU.mult), reads=[Bpkk, Bbe, BDTs], writes=[BM])
                        if main:
                            op(k.pool, lambda h: h.tensor_tensor(out=dm, in0=dm, in1=MI(t), op=ALU.mult), reads=[Bdm, Bmk, BDTs], writes=[Bdm])
                            p_qk, Bpqk = pst()
                            op(k.pe, lambda h: h.matmul(p_qk, lhsT=kT[:, cs0:cs0 + 128], rhs=qT[:, cs0:cs0 + 128], start=True, stop=True), reads=[BkT, BqT], writes=[Bpqk])
                            qk_, Bqk_ = LL["qkm"][par]
                            op(k.dve, lambda h: h.tensor_tensor(out=qk_[:], in0=p_qk, in1=dm, op=ALU.mult), reads=[Bpqk, Bdm], writes=[Bqk_])
                            qd, Bqd = LL["qdec"][par]
                            op(k.pool, lambda h: h.tensor_tensor(out=qd[:], in0=qT[:, cs0:cs0 + 128], in1=eg[:], op=ALU.mult), reads=[BqT, Beg], writes=[Bqd])
                        p_n, Bpn = pst()
                        op(k.pe, lambda h: h.transpose(p_n, M, ident[:]), reads=[BM, Bc], writes=[Bpn])
                        N, BN = tmp()
                        copy_ps(k.act, N, p_n, [Bpn], [BN])
                        Q, BQ = tmp()
                        op(k.dve, lambda h: h.tensor_tensor(out=Q, in0=ident[:], in1=M, op=ALU.subtract), reads=[Bc, BM], writes=[BQ])
                        for l in range(1, 7):
                            p1, Bp1 = pst()
                            op(k.pe, lambda h: h.matmul(p1, lhsT=M, rhs=N, start=True, stop=True), reads=[BM, BN], writes=[Bp1])
                            Nn, BNn = tmp()
                            copy_ps(k.act, Nn, p1, [Bp1], [BNn])
                            if l <= 5:
                                p2, Bp2 = pst()
                                op(k.pe, lambda h: h.matmul(p2, lhsT=N, rhs=M, start=True, stop=True), reads=[BM, BN], writes=[Bp2])
                                Mn, BMn = tmp()
                                copy_ps(k.dve, Mn, p2, [Bp2], [BMn])
                            p3, Bp3 = pst()
                            op(k.pe, lambda h: h.matmul(p3, lhsT=Nn, rhs=Q, start=True, stop=True), reads=[BNn, BQ], writes=[Bp3])
                            if l < 6:
                                Qn, BQn = tmp()
                            else:
                                Qn, BQn = LL["Q"][par][0][:], LL["Q"][par][1]
                            op(k.dve, lambda h: h.tensor_tensor(out=Qn, in0=p3, in1=Q, op=ALU.add), reads=[Bp3, BQ], writes=[BQn])
                            N, BN, Q, BQ = Nn, BNn, Qn, BQn
                            if l <= 5:
                                M, BM = Mn, BMn
                        p_t, Bpt = pst()
                        op(k.pe, lambda h: h.transpose(p_t, kT[:, cs0:cs0 + 128], ident[:]), reads=[BkT, Bc], writes=[Bpt])
                        ket, Bket = LL["kend"][par]
                        op(k.dve, lambda h: h.tensor_scalar(out=ket[:], in0=p_t, scalar1=kecol, scalar2=None, op0=ALU.mult), reads=[Bpt, Bke], writes=[Bket])
                        p_t2, Bpt2 = pst()
                        op(k.pe, lambda h: h.transpose(p_t2, vT[:, cs0:cs0 + 128], ident[:]), reads=[BvT, Bc], writes=[Bpt2])
                        vt, Bvt = LL["vtok"][par]
                        copy_ps(k.act, vt[:], p_t2, [Bpt2], [Bvt])

                    def seq(t):
                        par = t % 2
                        cs0 = t * 128
                        mixed = mtype(t) == 1
                        bcol = be_t[:, t, hd:hd + 1]; necol = ne_t[:, t, hd:hd + 1]
                        Q, BQ = LL["Q"][par]; ket, Bket = LL["kend"][par]; vt, Bvt = LL["vtok"][par]; eg, Beg = LL["egrow"][par]
                        p_ks, Bpks = pst()
                        if not mixed:
                            op(k.pe, lambda h: h.matmul(p_ks, lhsT=kT[:, cs0:cs0 + 128], rhs=S[:], start=True, stop=True), reads=[BkT, BS], writes=[Bpks])
                        else:
                            p_kst, Bpkst = pst()
                            op(k.pe, lambda h: h.matmul(p_kst[:, 0:64], lhsT=S[:], rhs=kT[:, cs0:cs0 + 64], start=True, stop=True), reads=[BkT, BS], writes=[Bpkst])
                            for b in range(NS):
                                op(k.pe, lambda h: h.matmul(p_kst[:, 64 + 4 * b:68 + 4 * b], lhsT=Sb[:, b, :], rhs=kT[:, cs0 + 64 + 4 * b:cs0 + 68 + 4 * b], start=True, stop=True),
                                   reads=[BkT, BSb], writes=[Bpkst])
                            kst, Bkst = tmp()
                            copy_ps(k.act, kst, p_kst, [Bpkst], [Bkst])
                            op(k.pe, lambda h: h.transpose(p_ks, kst, ident[:]), reads=[Bkst, Bc], writes=[Bpks])
                        rhsn, Brhsn = tmp()
                        op(k.dve, lambda h: h.scalar_tensor_tensor(out=rhsn, in0=p_ks, scalar=necol, in1=vt[:], op0=ALU.mult, op1=ALU.add), reads=[Bpks, Bne, Bvt], writes=[Brhsn])
                        p_u, Bpu = pst()
                        op(k.pe, lambda h: h.matmul(p_u, lhsT=Q[:], rhs=rhsn, start=True, stop=True), reads=[BQ, Brhsn], writes=[Bpu])
                        u_, Bu = LL["u"][par]
                        u = u_[:]
                        op(k.dve, lambda h: h.tensor_scalar(out=u, in0=p_u, scalar1=bcol, scalar2=None, op0=ALU.mult), reads=[Bpu, Bbe], writes=[Bu])
                        if main:
                            qd, Bqd = LL["qdec"][par]; qk_, Bqk_ = LL["qkm"][par]
                            p_a, Bpa = pst()
                            if not mixed:
                                op(k.pe, lambda h: h.matmul(p_a, lhsT=S[:], rhs=qd[:], start=True, stop=True), reads=[BS, Bqd], writes=[Bpa])
                            else:
                                op(k.pe, lambda h: h.matmul(p_a[:, 0:64], lhsT=S[:], rhs=qd[:, 0:64], start=True, stop=True), reads=[BS, Bqd], writes=[Bpa])
                                for b in range(NS):
                                    op(k.pe, lambda h: h.matmul(p_a[:, 64 + 4 * b:68 + 4 * b], lhsT=Sb[:, b, :], rhs=qd[:, 64 + 4 * b:68 + 4 * b], start=True, stop=True),
                                       reads=[BSb, Bqd], writes=[Bpa])
                            a_sb, Basb = tmp()
                            copy_ps(k.act, a_sb, p_a, [Bpa], [Basb])
                            p_b, Bpb = pst()
                            op(k.pe, lambda h: h.matmul(p_b, lhsT=u, rhs=qk_[:], start=True, stop=True), reads=[Bu, Bqk_], writes=[Bpb])
                            op(k.dve, lambda h: h.tensor_tensor(out=oT[:, cs0:cs0 + 128], in0=p_b, in1=a_sb, op=ALU.add), reads=[Bpb, Basb], writes=[BoT])
                        p_s, Bps_ = pst()
                        if not mixed:
                            op(k.pe, lambda h: h.matmul(p_s, lhsT=ket[:], rhs=u, start=True, stop=True), reads=[Bket, Bu], writes=[Bps_])
                            op(k.dve, lambda h: h.scalar_tensor_tensor(out=S[:], in0=S[:], scalar=eg[:, 127:128], in1=p_s, op0=ALU.mult, op1=ALU.add), reads=[BS, Beg, Bps_], writes=[BS])
                        else:
                            op(k.pe, lambda h: h.matmul(p_s, lhsT=ket[0:64, :], rhs=u_[0:64, :], start=True, stop=True), reads=[Bket, Bu], writes=[Bps_])
                            op(k.dve, lambda h: h.scalar_tensor_tensor(out=S[:], in0=S[:], scalar=eg[:, 63:64], in1=p_s, op0=ALU.mult, op1=ALU.add), reads=[BS, Beg, Bps_], writes=[BS])
                            for b in range(NS):
                                km, Bkm = tmp()
                                op(k.pool, lambda h: h.tensor_scalar(out=km, in0=ket[:], scalar1=rm[:, b:b + 1], scalar2=None, op0=ALU.mult), reads=[Bket, Brm], writes=[Bkm])
                                p_sb, Bpsb = pst()
                                op(k.pe, lambda h: h.matmul(p_sb, lhsT=km, rhs=u, start=True, stop=True), reads=[Bkm, Bu], writes=[Bpsb])
                                op(k.dve, lambda h: h.scalar_tensor_tensor(out=Sbn[:, b, :], in0=Sb[:, b, :], scalar=eg[:, 67 + 4 * b:68 + 4 * b], in1=p_sb, op0=ALU.mult, op1=ALU.add),
                                   reads=[BSb, Beg, Bpsb], writes=[BSbn])

                    pre(0)
                    for t in range(ntile):
                        if t + 1 < ntile:
                            pre(t + 1)
                        seq(t)
                    if main:
                        dma(k.sp, GDP[hd], S[:], reads=[BS], cb=BS)
                        dma(k.sp, GDS[:, hd].rearrange("b p d -> p b d"), Sbn[:], reads=[BSbn], cb=BSbn)
                        colsum_rs(oT[:], BoT, 1.0 / 128, cst["eps"][:, 0:1])
                        op(k.act, lambda h: h.activation(out=zT[:], in_=zT[:], func=AF.Silu), reads=[BzT], writes=[BzT])
                        op(k.dve, lambda h: h.scalar_tensor_tensor(out=oT[:], in0=oT[:], scalar=gon_t[:, 0:1], in1=rnt[:], op0=ALU.mult, op1=ALU.mult), reads=[BoT, Bgon, Brnt], writes=[BoT])
                        op(k.dve, lambda h: h.tensor_tensor(out=mixo[:], in0=oT[:], in1=zT[:], op=ALU.mult), reads=[BoT, BzT], writes=[Bmixo])
                        dma(k.sp, MIXT[hd], mixo[:], reads=[Bmixo], writes=[BMIX[hd]], cb=Bmixo)
                    else:
                        dma(k.sp, SPRE[hd], S[:], reads=[BS], writes=[BSPRE[hd]], cb=BS)
            with Phase(k) as ph:
                rd, Brd = ph.sb([128, 8, 2, 128], F32, "rd")
                dma(k.sp, rd[:], rdt.rearrange("h m p i -> p h m i"), writes=[Brd], cb=Brd)
                rm, Brm = ph.sb([128, NS], F32, "rm")
                dma(k.sp, rm[:], rowmask, writes=[Brm], cb=Brm)
                rke, Brke = ph.sb([128, 8, ntile], F32, "rke")
                dma(k.sp, rke[:], rkend if main else rkendp, writes=[Brke], cb=Brke)
                cs_t, Bcs = ph.sb([128, 2, Tn], F32, "cs")
                dma(k.sp, cs_t[:], (cs_m if main else cs_p).rearrange("c p t -> p c t"), writes=[Bcs], cb=Bcs)
                wide, Bwide = ph.ps([128, 3, 512], F32, "wide")
                smalls = []
                for _ in range(5):
                    bank, _b = ph.ps([128, 512], F32, "bank")
                    _b.excl = True
                    smalls.append((bank[:, 0:256], _b))
                pring = Ring(smalls)
                tring = Ring([ph.sb([128, 256], F32, "tmp") for _ in range(12)])

                def pst():
                    return pring.get()

                def tmp():
                    t_, b_ = tring.get()
                    return t_, b_

                raw = {nm: ph.sb([128, 2, Tn], F32, "r" + nm) for nm in (("q", "k", "v", "g") if main else ("k", "v"))}
                rot = {nm: ph.sb([128, 2, Tn], F32, "o" + nm) for nm in (("q", "k") if main else ("k",))}
                t1, Bt1 = ph.sb([128, Tn], F32, "t1")
                RL = {nm: [ph.sb([128, 256], F32, "rl" + nm) for _ in range(2)] for nm in ("ket", "vt", "qkm")}
                Sr, BSr = ph.sb([128, 2, 256], F32, "Sr")
                if main:
                    qdT, BqdT = ph.sb([128, 2, Tn], F32, "qdT"); qtab, Bqtab = ph.sb([128, Tn], F32, "qtab")
                    oR, BoR = ph.sb([128, 2, Tn], F32, "oR"); sqr, Bsqr = ph.sb([128, Tn], F32, "sqr")
                    mu, Bmu = ph.sb([128, Tn], F32, "mu"); rs_, Brs = ph.sb([128, Tn], F32, "rs")
                    mixo, Bmixo = ph.sb([128, 2, Tn], BF16, "mixo")
                    Srb = Ring([ph.sb([128, 2, 256], F32, "Srb") for _ in range(NS)])
                    Srn = Ring([ph.sb([128, 2, 256], F32, "Srn") for _ in range(3)])
                base = {"q": CRQ, "k": CRK, "v": CRV, "g": CRG}
                for hd in range(8):
                    gam = LOGG[hd]
                    for nm, (r_t, Br) in raw.items():
                        for c in range(2):
                            ci = sidx(base[nm] + 2 * hd + c)
                            dma(k.sp, r_t[:, c, :], SRC[ci], reads=[BSRC[ci]], writes=[Br], cb=Br)
                    for nm, (o_t, Bo) in rot.items():
                        r_t, Br = raw[nm]
                        sc = 1.0 if nm == "q" else 1.0 / 16.0
                        op(k.dve, lambda h: h.tensor_tensor(out=o_t[:, 0, :], in0=r_t[:, 0, :], in1=cs_t[:, 0, :], op=ALU.mult), reads=[Br, Bcs], writes=[Bo])
                        op(k.pool, lambda h: h.tensor_tensor(out=t1[:], in0=r_t[:, 1, :], in1=cs_t[:, 1, :], op=ALU.mult), reads=[Br, Bcs], writes=[Bt1])
                        op(k.dve, lambda h: h.tensor_tensor(out=o_t[:, 0, :], in0=o_t[:, 0, :], in1=t1[:], op=ALU.subtract), reads=[Bo, Bt1], writes=[Bo])
                        op(k.dve, lambda h: h.tensor_tensor(out=o_t[:, 1, :], in0=r_t[:, 0, :], in1=cs_t[:, 1, :], op=ALU.mult), reads=[Br, Bcs], writes=[Bo])
                        op(k.pool, lambda h: h.tensor_tensor(out=t1[:], in0=r_t[:, 1, :], in1=cs_t[:, 0, :], op=ALU.mult), reads=[Br, Bcs, Bo], writes=[Bt1])
                        op(k.dve, lambda h: h.tensor_tensor(out=o_t[:, 1, :], in0=o_t[:, 1, :], in1=t1[:], op=ALU.add), reads=[Bo, Bt1], writes=[Bo])
                        if sc != 1.0:
                            op(k.act, lambda h: h.activation(out=o_t[:], in_=o_t[:], func=AF.Copy, scale=sc), reads=[Bo], writes=[Bo])
                    kR, BkR = rot["k"]; vR, BvR = raw["v"]
                    if main:
                        qR, BqR = rot["q"]
                        dma(k.sp, qtab[:], rtab[hd, 0:1, :].partition_broadcast(128), writes=[Bqtab], cb=Bqtab)
                        for c in range(2):
                            op(k.pool, lambda h: h.tensor_tensor(out=qdT[:, c, :], in0=qR[:, c, :], in1=qtab[:], op=ALU.mult), reads=[BqR, Bqtab], writes=[BqdT])
                        dma(k.sp, Sr[:], RPRE[hd].rearrange("c p d -> p c d"), reads=[BRPRE[hd]], writes=[BSr], cb=BSr)
                    else:
                        op(k.pool, lambda h: h.memset(Sr[:], 0.0), writes=[BSr])
                    for t in range(ntile):
                        cs0 = t * 128
                        mixed = main and t == 8
                        ket, Bket = RL["ket"][t % 2]; vt, Bvt = RL["vt"][t % 2]
                        for c in range(2):
                            p_, Bp_ = pst()
                            op(k.pe, lambda h: h.transpose(p_[:, 0:128], kR[:, c, cs0:cs0 + 128], ident[:]), reads=[BkR, Bc], writes=[Bp_])
                            op(k.dve, lambda h: h.tensor_scalar(out=ket[:, c * 128:(c + 1) * 128], in0=p_[:, 0:128], scalar1=rke[:, hd, t:t + 1], scalar2=None, op0=ALU.mult),
                               reads=[Bp_, Brke], writes=[Bket])
                            p2, Bp2 = pst()
                            op(k.pe, lambda h: h.transpose(p2[:, 0:128], vR[:, c, cs0:cs0 + 128], ident[:]), reads=[BvR, Bc], writes=[Bp2])
                            copy_ps(k.act, vt[:, c * 128:(c + 1) * 128], p2[:, 0:128], [Bp2], [Bvt])
                        if main:
                            p_qk, Bpqk = pst()
                            for c in range(2):
                                op(k.pe, lambda h: h.matmul(p_qk[:, 0:128], lhsT=kR[:, c, cs0:cs0 + 128], rhs=qR[:, c, cs0:cs0 + 128], start=(c == 0), stop=(c == 1)),
                                   reads=[BkR, BqR], writes=[Bpqk])
                            qkm, Bqkm = RL["qkm"][t % 2]
                            op(k.dve, lambda h: h.tensor_tensor(out=qkm[:, 0:128], in0=p_qk[:, 0:128], in1=rd[:, hd, 1 if mixed else 0, :], op=ALU.mult), reads=[Bpqk, Brd], writes=[Bqkm])
                            if mixed:
                                rings_b = []
                                for b in range(NS):
                                    sb_t, Bsb = Srb.get()
                                    dma(k.sp, sb_t[:], srt[b, hd].rearrange("(c p) d -> p c d", p=128), writes=[Bsb], cb=Bsb)
                                    rings_b.append((sb_t, Bsb))
                            for c in range(2):
                                p_a, Bpa = pst()
                                if not mixed:
                                    for kc in range(2):
                                        op(k.pe, lambda h: h.matmul(p_a[:, 0:128], lhsT=Sr[:, kc, c * 128:(c + 1) * 128], rhs=qdT[:, kc, cs0:cs0 + 128], start=(kc == 0), stop=(kc == 1)),
                                           reads=[BSr, BqdT], writes=[Bpa])
                                else:
                                    for kc in range(2):
                                        op(k.pe, lambda h: h.matmul(p_a[:, 0:64], lhsT=Sr[:, kc, c * 128:(c + 1) * 128], rhs=qdT[:, kc, cs0:cs0 + 64], start=(kc == 0), stop=(kc == 1)),
                                           reads=[BSr, BqdT], writes=[Bpa])
                                    for b in range(NS):
                                        sb_t, Bsb = rings_b[b]
                                        for kc in range(2):
                                            op(k.pe, lambda h: h.matmul(p_a[:, 64 + 4 * b:68 + 4 * b], lhsT=sb_t[:, kc, c * 128:(c + 1) * 128],
                                                                        rhs=qdT[:, kc, cs0 + 64 + 4 * b:cs0 + 68 + 4 * b], start=(kc == 0), stop=(kc == 1)),
                                               reads=[Bsb, BqdT], writes=[Bpa])
                                a_sb, Basb = tmp()
                                copy_ps(k.act, a_sb[:, 0:128], p_a[:, 0:128], [Bpa], [Basb])
                                p_b, Bpb = pst()
                                op(k.pe, lambda h: h.matmul(p_b[:, 0:128], lhsT=vt[:, c * 128:(c + 1) * 128], rhs=qkm[:, 0:128], start=True, stop=True), reads=[Bvt, Bqkm], writes=[Bpb])
                                op(k.dve, lambda h: h.tensor_tensor(out=oR[:, c, cs0:cs0 + 128], in0=p_b[:, 0:128], in1=a_sb[:, 0:128], op=ALU.add), reads=[Bpb, Basb], writes=[BoR])
                        Cn = 64 if mixed else 128
                        for kc in range(2):
                            p_s, Bps_ = pst()
                            op(k.pe, lambda h: h.matmul(p_s, lhsT=ket[0:Cn, kc * 128:(kc + 1) * 128], rhs=vt[0:Cn, :], start=True, stop=True), reads=[Bket, Bvt], writes=[Bps_])
                            op(k.dve, lambda h: h.scalar_tensor_tensor(out=Sr[:, kc, :], in0=Sr[:, kc, :], scalar=float(math.exp(gam * Cn)), in1=p_s, op0=ALU.mult, op1=ALU.add),
                               reads=[BSr, Bps_], writes=[BSr])
                        if mixed:
                            for b in range(NS):
                                sb_t, Bsb = rings_b[b]
                                sn_t, Bsn = Srn.get()
                                km, Bkm = tmp()
                                op(k.pool, lambda h: h.tensor_scalar(out=km[:], in0=ket[:], scalar1=rm[:, b:b + 1], scalar2=None, op0=ALU.mult), reads=[Bket, Brm], writes=[Bkm])
                                for kc in range(2):
                                    p_s, Bps_ = pst()
                                    op(k.pe, lambda h: h.matmul(p_s, lhsT=km[:, kc * 128:(kc + 1) * 128], rhs=vt[:], start=True, stop=True), reads=[Bkm, Bvt], writes=[Bps_])
                                    op(k.dve, lambda h: h.scalar_tensor_tensor(out=sn_t[:, kc, :], in0=sb_t[:, kc, :], scalar=float(math.exp(gam * 4)), in1=p_s, op0=ALU.mult, op1=ALU.add),
                                       reads=[Bsb, Bps_], writes=[Bsn])
                                dma(k.sp, RTS[b, hd].rearrange("c p d -> p c d"), sn_t[:], reads=[Bsn], cb=Bsn)
                    if main:
                        dma(k.sp, RTP[hd].rearrange("c p d -> p c d"), Sr[:], reads=[BSr], cb=BSr)
                        for ti, (t0, tw_) in enumerate(tts):
                            for c in range(2):
                                op(k.pe, lambda h: h.matmul(wide[:, ti, 0:tw_], lhsT=ones_f[:], rhs=oR[:, c, t0:t0 + tw_], start=(c == 0), stop=(c == 1)), reads=[BoR, Bc], writes=[Bwide])
                        muv = mu[:].rearrange("p (a b) -> p a b", a=nt)
                        op(k.act, lambda h: h.activation(out=muv, in_=wide[:, :, 0:tw], func=AF.Copy, scale=1.0 / 256), reads=[Bwide], writes=[Bmu])
                        for c in range(2):
                            op(k.dve, lambda h: h.tensor_tensor(out=oR[:, c, :], in0=oR[:, c, :], in1=mu[:], op=ALU.subtract), reads=[BoR, Bmu], writes=[BoR])
                        for c in range(2):
                            op(k.act, lambda h: h.activation(out=sqr[:], in_=oR[:, c, :], func=AF.Square), reads=[BoR], writes=[Bsqr])
                            for ti, (t0, tw_) in enumerate(tts):
                                op(k.pe, lambda h: h.matmul(wide[:, ti, 0:tw_], lhsT=ones_f[:], rhs=sqr[:, t0:t0 + tw_], start=(c == 0), stop=(c == 1)), reads=[Bsqr, Bc], writes=[Bwide])
                        rsv = rs_[:].rearrange("p (a b) -> p a b", a=nt)
                        op(k.act, lambda h: h.activation(out=rsv, in_=wide[:, :, 0:tw], func=AF.Sqrt, bias=cst["eps"][:, 0:1], scale=1.0 / 256), reads=[Bwide, Bc], writes=[Brs])
                        op(k.dve, lambda h: h.reciprocal(out=rs_[:], in_=rs_[:]), reads=[Brs], writes=[Brs])
                        gR, BgR = raw["g"]
                        op(k.act, lambda h: h.activation(out=gR[:], in_=gR[:], func=AF.Silu), reads=[BgR], writes=[BgR])
                        for c in range(2):
                            op(k.dve, lambda h: h.tensor_tensor(out=oR[:, c, :], in0=oR[:, c, :], in1=rs_[:], op=ALU.mult), reads=[BoR, Brs], writes=[BoR])
                            op(k.dve, lambda h: h.tensor_tensor(out=mixo[:, c, :], in0=oR[:, c, :], in1=gR[:, c, :], op=ALU.mult), reads=[BoR, BgR], writes=[Bmixo])
                            dma(k.sp, MIXT[16 + 2 * hd + c], mixo[:, c, :], reads=[Bmixo], writes=[BMIX[16 + 2 * hd + c]], cb=Bmixo)
                    else:
                        dma(k.sp, RPRE[hd].rearrange("c p d -> p c d"), Sr[:], reads=[BSr], writes=[BRPRE[hd]], cb=BSr)

        if stop_after != "proj":
            mixer_phase(False)
            mixer_phase(True)
        P.ctx = dict(cst=cst, gains_t=gains_t, hm_t=hm_t, X=X, BX=BX, BXin=BXin, MIXT=MIXT, BMIX=BMIX, AT=AT, BAT=BAT,
                     copy_ps=copy_ps, evac_alt=evac_alt)
        if stop_after not in ("proj", "mixer"):
            build_rest(P)
        k.barrier()
    return nc


def build_rest(P):
    k = P.k
    nc = P.nc
    op, dma = k.op, k.dma
    io = P.io
    c_ = P.ctx
    cst, gains_t, hm_t = c_["cst"], c_["gains_t"], c_["hm_t"]
    X, BX, BXin, MIXT, BMIX, AT, BAT = c_["X"], c_["BX"], c_["BXin"], c_["MIXT"], c_["BMIX"], c_["AT"], c_["BAT"]
    copy_ps = c_["copy_ps"]
    Bc = cst["B"]
    xT = io["xT"]
    nt, tw = 3, 384

    with Phase(k) as ph:
        mT, BmT = ph.sb([128, KC, T], BF16, "mT")
        for c in range(KC):
            dma(k.sp, mT[:, c, :], MIXT[c], reads=[BMIX[c]], writes=[BmT], cb=BmT)
        blocks, chunks = merge_blocks([(i * 128, 128) for i in range(KC)])
        epi = P.make_xupd(ph, xT, BXin, X, BX, T, TT_MAIN)
        P.gemm(ph, io["w_out"], KC, lambda kc, t0, tw_: mT[:, kc, t0:t0 + tw_], BmT, TT_MAIN, blocks, chunks, epi, nslot=2, NB=512)

    def ffn(l):
        with Phase(k) as ph:
            hT, BhT = ph.sb([128, KC, T], BF16, "hT")
            with Phase(k) as ph2:
                P.rmsnorm(ph2, X, BX, T, TT_MAIN, gains_t, 3 * l + 1, hT, BhT, cst)
            fw, Bfw = ph.sb([128, 172, 4], F32, "fw")
            dma(k.sp, fw[:], io["fcw"][:, l], writes=[Bfw], cb=Bfw)
            EXW = 2 + TPR + NS * 6
            exg = Ring([ph.sb([128, EXW], F32, "exg") for _ in range(2)])
            exv = Ring([ph.sb([128, EXW], F32, "exv") for _ in range(2)])
            cg = Ring([ph.sb([128, T], F32, "cg") for _ in range(2)])
            cv = Ring([ph.sb([128, T], F32, "cv") for _ in range(2)])
            ao = Ring([ph.sb([128, T], BF16, "ao") for _ in range(2)])
            blocks, chunks = [], []
            for j in range(43):
                blocks.append((256 * j, 256)); blocks.append((DFF + 256 * j, 256))
                for o in range(2):
                    chunks.append((2 * j, 128 * o, 128, ("g", 2 * j + o)))
                    chunks.append((2 * j + 1, 128 * o, 128, ("v", 2 * j + o)))
            cur = {}

            def conv(ext, Bext, ch, out, Bout, ps, Bps):
                es_ = ext[:, 2 + TPR:].rearrange("p (b j) -> p b j", j=6)
                op(k.pool, lambda h: h.memset(ext[:, 0:2], 0.0), writes=[Bext])
                dma(k.sp, es_[:, :, 0:2], io["sff"][l, ch], writes=[Bext], cb=Bext)
                op(k.act, lambda h: h.activation(out=ext[:, 2:2 + 768].rearrange("p (a b) -> p a b", a=2), in_=ps[:, 0:2, 0:384], func=AF.Copy), reads=[Bps], writes=[Bext])
                op(k.act, lambda h: h.activation(out=ext[:, 770:770 + 320], in_=ps[:, 2, 0:320], func=AF.Copy), reads=[Bps], writes=[Bext])
                op(k.act, lambda h: h.activation(out=es_[:, :, 2:6], in_=ps[:, 2, 320:384].rearrange("p (b j) -> p b j", j=4), func=AF.Copy), reads=[Bps], writes=[Bext])
                dma(k.sp, io["FFP"][l, ch], ext[:, TPR:TPR + 2], reads=[Bext], cb=Bext)
                dma(k.sp, io["FFS"][l, ch], es_[:, :, 4:6], reads=[Bext], cb=Bext)
                op(k.dve, lambda h: h.tensor_scalar(out=out[:, 0:TPR], in0=ext[:, 0:TPR], scalar1=fw[:, ch, 0:1], scalar2=fw[:, ch, 3:4], op0=ALU.mult, op1=ALU.add),
                   reads=[Bext, Bfw], writes=[Bout])
                for j in (1, 2):
                    op(k.dve, lambda h: h.scalar_tensor_tensor(out=out[:, 0:TPR], in0=ext[:, j:j + TPR], scalar=fw[:, ch, j:j + 1], in1=out[:, 0:TPR], op0=ALU.mult, op1=ALU.add),
                       reads=[Bext, Bfw, Bout], writes=[Bout])
                os_ = out[:, TPR:].rearrange("p (b j) -> p b j", j=4)
                op(k.pool, lambda h: h.tensor_scalar(out=os_, in0=es_[:, :, 0:4], scalar1=fw[:, ch, 0:1], scalar2=fw[:, ch, 3:4], op0=ALU.mult, op1=ALU.add),
                   reads=[Bext, Bfw], writes=[Bout])
                for j in (1, 2):
                    op(k.dve, lambda h: h.scalar_tensor_tensor(out=os_, in0=es_[:, :, j:j + 4], scalar=fw[:, ch, j:j + 1], in1=os_, op0=ALU.mult, op1=ALU.add),
                       reads=[Bext, Bfw, Bout], writes=[Bout])

            def epi(tag, ps, Bps, width):
                kind, m = tag
                if kind == "g":
                    ext, Bext = exg.get(); out, Bout = cg.get()
                    conv(ext, Bext, m, out, Bout, ps, Bps)
                    op(k.act, lambda h: h.activation(out=out[:], in_=out[:], func=AF.Silu), reads=[Bout], writes=[Bout])
                    cur["g"] = (out, Bout)
                else:
                    ext, Bext = exv.get(); out, Bout = cv.get()
                    conv(ext, Bext, NFC + m, out, Bout, ps, Bps)
                    g_t, Bg = cur["g"]
                    a_t, Ba = ao.get()
                    op(k.pool, lambda h: h.tensor_tensor(out=a_t[:], in0=g_t[:], in1=out[:], op=ALU.mult), reads=[Bg, Bout], writes=[Ba])
                    dma(k.sp, AT[m], a_t[:], reads=[Ba], writes=[BAT[m]], cb=Ba)

            P.gemm(ph, io["w_up"][l], KC, lambda kc, t0, tw_: hT[:, kc, t0:t0 + tw_], BhT, TT_MAIN, blocks, chunks, epi, nslot=4, NB=256)
        for half in range(2):
            with Phase(k) as ph:
                t_off = 576 * half
                aT, BaT = ph.sb([128, NFC, 576], BF16, "aT")
                for m in range(NFC):
                    dma(k.sp, aT[:, m, :], AT[m][:, t_off:t_off + 576], reads=[BAT[m]], writes=[BaT], cb=BaT)
                tts = [(0, 288), (288, 288)]
                blocks, chunks = merge_blocks([(i * 128, 128) for i in range(KC)], maxw=128)
                epi = P.make_xupd(ph, X, BX, X, BX, 576, tts, t_off=t_off)
                P.gemm(ph, io["w_down"][l], NFC, lambda kc, t0, tw_: aT[:, kc, t0:t0 + tw_], BaT, tts, blocks, chunks, epi, nslot=3, NB=128)

    def ple(l):
        with Phase(k) as ph:
            hT, BhT = ph.sb([128, KC, T], BF16, "hT")
            with Phase(k) as ph2:
                P.rmsnorm(ph2, X, BX, T, TT_MAIN, gains_t, 3 * l + 2, hT, BhT, cst)
            pf, Bpf = ph.sb([128, 2, T], F32, "pf"); pb, Bpb = ph.sb([128, 2, T], BF16, "pb")
            dma(k.sp, pf[:], io["pT"][l].rearrange("c p t -> p c t"), writes=[Bpf], cb=Bpf)
            op(k.dve, lambda h: h.tensor_copy(pb[:], pf[:]), reads=[Bpf], writes=[Bpb])
            blocks, chunks = merge_blocks([(i * 128, 128) for i in range(KC)], maxw=256)
            epi = P.make_xupd(ph, X, BX, X, BX, T, TT_MAIN, gate=True)
            P.gemm(ph, io["w_pg"][l], KC, lambda kc, t0, tw_: hT[:, kc, t0:t0 + tw_], BhT, TT_MAIN, blocks, chunks, epi, nslot=2, NB=256,
                   extra=(io["w_pp"][l], 2, lambda kc, t0, tw_: pb[:, kc, t0:t0 + tw_], Bpb))

    def pool_mixer():
        with Phase(k) as ph:
            pooled, Bpooled = ph.sb([128, KC, T], BF16, "pooled")
            with Phase(k) as ph2:
                pss, Bpss = ph2.ps([128, 3, 512], F32, "nps")
                xs = Ring([ph2.sb([128, T], F32, "xs") for _ in range(3)])
                sq = Ring([ph2.sb([128, T], BF16, "sq") for _ in range(2)])
                rstd, Brstd = ph2.sb([128, T], F32, "rstd")
                for c in range(KC):
                    x_t, Bx = xs.get()
                    dma(k.sp, x_t[:], X[c], reads=[BX[c]], writes=[Bx], cb=Bx)
                    op(k.dve, lambda h: h.tensor_scalar(out=x_t[:, 0:64], in0=x_t[:, 0:64], scalar1=hm_t[:, 0:1], scalar2=None, op0=ALU.mult), reads=[Bx, Bc], writes=[Bx])
                    dma(k.sp, X[c][:, 0:64], x_t[:, 0:64], reads=[Bx], writes=[BX[c]], cb=Bx)
                    s_t, Bs = sq.get()
                    op(k.act, lambda h: h.activation(out=s_t[:], in_=x_t[:], func=AF.Square), reads=[Bx], writes=[Bs])
                    for ti, (t0, tw_) in enumerate(TT_MAIN):
                        op(k.pe, lambda h: h.matmul(pss[:, ti, 0:tw_], lhsT=cst["ones_bf"][:], rhs=s_t[:, t0:t0 + tw_], start=(c == 0), stop=(c == KC - 1)), reads=[Bs, Bc], writes=[Bpss])
                op(k.act, lambda h: h.activation(out=rstd[:].rearrange("p (a b) -> p a b", a=3), in_=pss[:, :, 0:384], func=AF.Sqrt, bias=cst["eps"][:, 0:1], scale=1.0 / D),
                   reads=[Bpss, Bc], writes=[Brstd])
                op(k.dve, lambda h: h.reciprocal(out=rstd[:], in_=rstd[:]), reads=[Brstd], writes=[Brstd])
                iv2, Biv = ph2.sb([128, 4 * T], F32, "iv")
                dma(k.sp, iv2[:], io["pinv"][0:1, :].partition_broadcast(128), writes=[Biv], cb=Biv)
                iv = iv2[:].rearrange("p (g t) -> p g t", g=4)
                LE = 15 + TPR
                EW = LE + NS * 19
                Eb = Ring([ph2.sb([128, EW], F32, "E") for _ in range(2)])
                Ub = Ring([ph2.sb([128, EW], F32, "U") for _ in range(2)])
                Vb = Ring([ph2.sb([128, EW], F32, "V") for _ in range(2)])
                for c in range(KC):
                    g = c // 8
                    x_t, Bx = xs.get()
                    dma(k.sp, x_t[:], X[c], reads=[BX[c]], writes=[Bx], cb=Bx)
                    E, BE = Eb.get(); U, BU = Ub.get(); V, BV = Vb.get()
                    Es = E[:, LE:].rearrange("p (b j) -> p b j", j=19)
                    op(k.pool, lambda h: h.memset(E[:, 0:15], 0.0), writes=[BE])
                    dma(k.sp, Es[:, :, 0:15], io["spl"][c], writes=[BE], cb=BE)
                    op(k.dve, lambda h: h.scalar_tensor_tensor(out=E[:, 15:LE], in0=x_t[:, 0:TPR], scalar=gains_t[:, 3, c:c + 1], in1=rstd[:, 0:TPR], op0=ALU.mult, op1=ALU.mult),
                       reads=[Bx, Brstd, Bc], writes=[BE])
                    op(k.dve, lambda h: h.scalar_tensor_tensor(out=Es[:, :, 15:19], in0=x_t[:, TPR:].rearrange("p (b j) -> p b j", j=4), scalar=gains_t[:, 3, c:c + 1],
                                                               in1=rstd[:, TPR:].rearrange("p (b j) -> p b j", j=4), op0=ALU.mult, op1=ALU.mult), reads=[Bx, Brstd, Bc], writes=[BE])
                    dma(k.sp, io["PLP"][c], E[:, LE - 15:LE], reads=[BE], cb=BE)
                    dma(k.sp, io["PLS"][c], Es[:, :, 4:19], reads=[BE], cb=BE)
                    src, Bsrc = E, BE
                    sh = 1
                    bufs = [(U, BU), (V, BV)]
                    for lev in range(g + 1):
                        dst, Bdst = bufs[lev % 2]
                        lo = 2 * sh - 1
                        op(k.dve, lambda h: h.tensor_tensor(out=dst[:, lo:LE], in0=src[:, lo:LE], in1=src[:, lo - sh:LE - sh], op=ALU.add), reads=[Bsrc], writes=[Bdst])
                        ds_ = dst[:, LE:].rearrange("p (b j) -> p b j", j=19); ss_ = src[:, LE:].rearrange("p (b j) -> p b j", j=19)
                        op(k.pool, lambda h: h.tensor_tensor(out=ds_[:, :, lo:19], in0=ss_[:, :, lo:19], in1=ss_[:, :, lo - sh:19 - sh], op=ALU.add), reads=[Bsrc], writes=[Bdst])
                        src, Bsrc = dst, Bdst
                        sh *= 2
                    op(k.dve, lambda h: h.tensor_tensor(out=src[:, 15:LE], in0=src[:, 15:LE], in1=iv[:, g, 0:TPR], op=ALU.mult), reads=[Bsrc, Biv], writes=[Bsrc])
                    op(k.dve, lambda h: h.tensor_tensor(out=pooled[:, c, 0:TPR], in0=src[:, 15:LE], in1=E[:, 15:LE], op=ALU.subtract), reads=[Bsrc, BE], writes=[Bpooled])
                    ss_ = src[:, LE:].rearrange("p (b j) -> p b j", j=19)
                    op(k.pool, lambda h: h.tensor_tensor(out=ss_[:, :, 15:19], in0=ss_[:, :, 15:19], in1=iv[:, g, TPR:].rearrange("p (b j) -> p b j", j=4), op=ALU.mult),
                       reads=[Bsrc, Biv], writes=[Bsrc])
                    op(k.pool, lambda h: h.tensor_tensor(out=pooled[:, c, TPR:].rearrange("p (b j) -> p b j", j=4), in0=ss_[:, :, 15:19], in1=Es[:, :, 15:19], op=ALU.subtract),
                       reads=[Bsrc, BE], writes=[Bpooled])
            for g in range(4):
                with Phase(k) as ph3:
                    blocks, chunks = merge_blocks([(i * 128, 128) for i in range(8)], maxw=512)
                    chunks = [(bi, off, w, 8 * g + tag) for (bi, off, w, tag) in chunks]
                    epi = P.make_xupd(ph3, X, BX, X, BX, T, TT_MAIN, scale_t=gains_t, scale_idx=7, cst=cst)
                    P.gemm(ph3, io["pool_w"][g], 8, lambda kc, t0, tw_, g=g: pooled[:, 8 * g + kc, t0:t0 + tw_], Bpooled, TT_MAIN, blocks, chunks, epi, nslot=2, NB=512)

    def final_norm():
        with Phase(k) as ph:
            stg = Ring([ph.sb([128, T], F32, "fo") for _ in range(2)])

            def outf(c, x_t, Bx, rstd, Brstd):
                o_t, Bo = stg.get()
                op(k.dve, lambda h: h.scalar_tensor_tensor(out=o_t[:], in0=x_t[:], scalar=gains_t[:, 6, c:c + 1], in1=rstd[:], op0=ALU.mult, op1=ALU.mult),
                   reads=[Bx, Brstd, Bc], writes=[Bo])
                dma(k.sp, io["YT"][c], o_t[:], reads=[Bo], cb=Bo)

            P.rmsnorm(ph, X, BX, T, TT_MAIN, gains_t, 6, None, None, cst, out_f32=outf)

    stop = P.stop_after
    if stop == "wout":
        return
    ffn(0)
    if stop == "ffn0":
        return
    ple(0)
    if stop == "ple0":
        return
    pool_mixer()
    if stop == "pool":
        return
    ffn(1)
    ple(1)
    final_norm()


def _fm(a):
    t, f = a.shape
    return np.ascontiguousarray(a.T.reshape(f // 128, 128, t))


def _consts(s):
    tok0 = 1024 * s
    idx = np.arange(128)
    MIf = (idx[:, None] <= idx[None, :]).astype(np.float32)
    MSf = (idx[:, None] < idx[None, :]).astype(np.float32)
    BLf = np.ones((128, 128), np.float32)
    blk = np.where(idx < 64, 0, 1 + (idx - 64) // 4)
    same = (blk[:, None] == blk[None, :])
    MIm = (same & (idx[:, None] <= idx[None, :])).astype(np.float32)
    MSm = (same & (idx[:, None] < idx[None, :])).astype(np.float32)
    BLm = same.astype(np.float32)
    masks = np.stack([MIf, MSf, BLf, MIm, MSm, BLm]).astype(np.float32)
    rowmask = np.zeros((128, NS), np.float32)
    for b in range(NS):
        rowmask[64 + 4 * b:68 + 4 * b, b] = 1.0
    gam = np.array([1.0 - 2.0 ** (-5 - h) for h in range(8)], np.float64)
    dif = (idx[None, :] - idx[:, None]).astype(np.float64)
    rdt = np.zeros((8, 2, 128, 128), np.float32)
    for h in range(8):
        rdt[h, 0] = np.where(MIf > 0, gam[h] ** np.maximum(dif, 0), 0.0)
        rdt[h, 1] = np.where(MIm > 0, gam[h] ** np.maximum(dif, 0), 0.0)
    col = np.arange(T)
    loc = np.where(col < 1024, col % 128, np.where(col < TPR, col - 1024, (col - TPR) % 4)).astype(np.float64)
    rtab = np.zeros((8, 2, T), np.float32)
    for h in range(8):
        rtab[h, 0] = gam[h] ** (loc + 1)
    rkend = np.zeros((128, 8, 9), np.float32)
    rkendp = np.zeros((128, 8, 8), np.float32)
    p = idx.astype(np.float64)
    last_mixed = np.where(idx < 64, 63 - p, 3 - (p - 64) % 4)
    for h in range(8):
        for t in range(9):
            rkend[:, h, t] = gam[h] ** (last_mixed if t == 8 else 127 - p)
        for t in range(8):
            rkendp[:, h, t] = gam[h] ** (127 - p)
    pos_m = np.where(col < TPR, tok0 - 64 + col, 16384 + (col - TPR) % 4).astype(np.float32)
    pos_p = (np.arange(TP) - 64).astype(np.float32)
    inv_freq = (np.float32(10000.0) ** (-np.arange(128, dtype=np.float32) / np.float32(128))).astype(np.float32)
    ang_m = (pos_m[None, :] * inv_freq[:, None]).astype(np.float32)
    ang_p = (pos_p[None, :] * inv_freq[:, None]).astype(np.float32)
    cs_m = np.stack([np.cos(ang_m), np.sin(ang_m)]).astype(np.float32)
    cs_p = np.stack([np.cos(ang_p), np.sin(ang_p)]).astype(np.float32)
    pinv = np.zeros((4, T), np.float32)
    posi = np.where(col < TPR, tok0 - 64 + col, 16384 + (col - TPR) % 4)
    for g in range(4):
        w = 2 ** (g + 1)
        cnt = np.minimum(w, np.maximum(posi + 1, 1)).astype(np.float32)
        pinv[g] = 1.0 / cnt
    return dict(masks=masks, rowmask=rowmask, rdt=rdt, rtab=rtab, rkend=rkend, rkendp=rkendp, cs_m=cs_m, cs_p=cs_p,
                pinv=pinv.reshape(1, 4 * T), hmask=np.full((128, 1), float(s), np.float32))


def make_in_maps(inp):
    f32 = np.float32
    g = lambda n: np.asarray(inp[n], dtype=f32)
    xp, xs_ = g("x_prompt"), g("x_sample")
    pp, psm = g("p_prompt"), g("p_sample")
    gains = np.stack([g("norm_mix")[0], g("norm_ffn")[0], g("norm_ple")[0], g("norm_mix")[1], g("norm_ffn")[1], g("norm_ple")[1],
                      g("norm_final"), g("pool_scale")[0]])
    gains = np.ascontiguousarray(gains.reshape(8, KC, 128).transpose(2, 0, 1))
    gcw = np.ascontiguousarray(g("gdn_conv_w")[0].reshape(4, 48, 128).transpose(2, 1, 0))
    fc = np.concatenate([g("ffn_conv_w"), g("ffn_conv_b")[:, None, :]], axis=1)
    fcw = np.ascontiguousarray(fc.reshape(2, 4, 172, 128).transpose(3, 0, 2, 1))
    ghv = np.zeros((1, 48), f32)
    ghv[0, 0:16] = g("gdn_a_log")[0]
    ghv[0, 16:32] = g("gdn_dt_bias")[0]
    shared = dict(gains=gains, w_in=g("w_in")[0], w_out=g("w_out")[0], pool_w=g("pool_w")[0], w_up=g("ffn_w_up"), w_down=g("ffn_w_down"),
                  w_pg=g("ple_w_gate"), w_pp=g("ple_w_proj"), gcw=gcw, fcw=fcw, ghv=ghv, gon=g("gdn_out_norm")[0].reshape(128, 1))
    sgc_a, sgd_a, srt_a, spl_a, sff_a = g("state_gdn_conv")[0], g("state_gdn")[0], g("state_ret")[0], g("state_pool")[0], g("state_ffn_conv")
    maps = []
    for c in range(8):
        b, s = c // 2, c % 2
        tok0 = 1024 * s
        halo = xp[b, tok0 - 64:tok0] if s == 1 else np.zeros((64, D), f32)
        xm = np.concatenate([halo, xp[b, tok0:tok0 + 1024], xs_[16 * c:16 * c + 16].reshape(64, D)], axis=0)
        xpre = np.zeros((TP, D), f32)
        if s == 1:
            xpre[64:] = xp[b, 0:960]
        pm = []
        for l in range(2):
            ph_ = pp[l, b, tok0 - 64:tok0] if s == 1 else np.zeros((64, 256), f32)
            pm.append(_fm(np.concatenate([ph_, pp[l, b, tok0:tok0 + 1024], psm[l, 16 * c:16 * c + 16].reshape(64, 256)], axis=0)))
        sl = slice(16 * c, 16 * c + 16)
        m = dict(shared)
        m.update(_consts(s))
        m.update(xT=_fm(xm), xpT=_fm(xpre), pT=np.stack(pm),
                 sgc=np.ascontiguousarray(sgc_a[sl].transpose(2, 0, 1).reshape(48, 128, NS, 3)),
                 sgd=np.ascontiguousarray(sgd_a[sl]), srt=np.ascontiguousarray(srt_a[sl]),
                 spl=np.ascontiguousarray(spl_a[sl].transpose(2, 0, 1).reshape(KC, 128, NS, 15)),
                 sff=np.ascontiguousarray(sff_a[:, sl].transpose(0, 3, 1, 2).reshape(2, 172, 128, NS, 2)))
        maps.append(m)
    return maps


def assemble(res):
    f32 = np.float32
    y_p = np.zeros((4, 2048, D), f32); y_s = np.zeros((128, 4, D), f32)
    gc_p = np.zeros((1, 4, 3, 6144), f32); gd_p = np.zeros((1, 4, 16, 128, 128), f32); rt_p = np.zeros((1, 4, 8, 256, 256), f32)
    pl_p = np.zeros((1, 4, 15, D), f32); ff_p = np.zeros((2, 4, 2, 2 * DFF), f32)
    gc_s = np.zeros((1, 128, 3, 6144), f32); gd_s = np.zeros((1, 128, 16, 128, 128), f32); rt_s = np.zeros((1, 128, 8, 256, 256), f32)
    pl_s = np.zeros((1, 128, 15, D), f32); ff_s = np.zeros((2, 128, 2, 2 * DFF), f32)
    for c in range(8):
        r = res[c]
        b, s = c // 2, c % 2
        sl = slice(16 * c, 16 * c + 16)
        yt = np.asarray(r["YT"]).reshape(D, T)
        y_p[b, 1024 * s:1024 * s + 1024] = yt[:, 64:TPR].T
        y_s[sl] = yt[:, TPR:].T.reshape(16, 4, D)
        gc_s[0, sl] = np.asarray(r["GCS"]).reshape(6144, NS, 3).transpose(1, 2, 0)
        gd_s[0, sl] = np.asarray(r["GDS"])
        rt_s[0, sl] = np.asarray(r["RTS"]).reshape(NS, 8, 256, 256)
        pl_s[0, sl] = np.asarray(r["PLS"]).reshape(D, NS, 15).transpose(1, 2, 0)
        ff_s[:, sl] = np.asarray(r["FFS"]).reshape(2, 2 * DFF, NS, 2).transpose(0, 2, 3, 1)
        if s == 1:
            gc_p[0, b] = np.asarray(r["GCP"]).reshape(6144, 3).T
            gd_p[0, b] = np.asarray(r["GDP"])
            rt_p[0, b] = np.asarray(r["RTP"]).reshape(8, 256, 256)
            pl_p[0, b] = np.asarray(r["PLP"]).reshape(D, 15).T
            ff_p[:, b] = np.asarray(r["FFP"]).reshape(2, 2 * DFF, 2).transpose(0, 2, 1)
    return (y_p, y_s, gc_p, gd_p, rt_p, pl_p, ff_p, gc_s, gd_s, rt_s, pl_s, ff_s)


def kernel(**inputs):
    nc = build()
    maps = make_in_maps(inputs)
    res = run_bass_kernel_spmd(nc, maps, core_ids=list(range(8)))
    return assemble(res.results)
```
